# Optimizing a Trainium2 kernel written in Bass

```python
import math
import jax, jax.numpy as jnp
from jax import lax
import numpy as np

D_MODEL = 1024
BATCH = 4
SEQ = 4096
DEPTH = 2
DEC_BATCH = 128
DEC_SEQ = 1
PAST_LEN = 2048
PAGE_SIZE = 128

N_MIXERS = 2
N_CONV_LAYERS = (DEPTH + 1) // 2
N_ATTN_LAYERS = DEPTH // 2
CONV_WIDTH = 31
N_HEADS = 16
HEAD_DIM = D_MODEL // N_HEADS
MOBA_BLOCK = 256
MOBA_TOPK = 3
MOBA_Q_CHUNK = 64
T5_BUCKETS = 32
T5_MAX_DIST = 128
PEER_HEADS = 8
PEER_N_KEYS = 128
PEER_N_EXPERTS = PEER_N_KEYS * PEER_N_KEYS
PEER_KEY_HALF = 64
PEER_TOPK = 16
PEER_CHUNK = 256
PLE_DIM = 256
EPS = 1e-6

kernel_name = 'moba_conformer_peer_hybrid_step'


def rms_norm(x, g):
    x32 = x.astype(jnp.float32)
    y = x32 * lax.rsqrt(jnp.mean(x32 * x32, axis=-1, keepdims=True) + EPS)
    return (y * g.astype(jnp.float32)).astype(x.dtype)


def layer_norm(x, g, b):
    x32 = x.astype(jnp.float32)
    mu = jnp.mean(x32, axis=-1, keepdims=True)
    xc = x32 - mu
    y = xc * lax.rsqrt(jnp.mean(xc * xc, axis=-1, keepdims=True) + EPS)
    return (y * g.astype(jnp.float32) + b.astype(jnp.float32)).astype(x.dtype)


def t5_bucket(dist):
    max_exact = T5_BUCKETS // 2
    df = jnp.maximum(dist, 1).astype(jnp.float32)
    large = max_exact + (jnp.log(df / max_exact) / math.log(T5_MAX_DIST / max_exact)
                         * (T5_BUCKETS - max_exact)).astype(jnp.int32)
    large = jnp.minimum(large, T5_BUCKETS - 1)
    return jnp.where(dist < max_exact, dist, large)


def conv_module(a, hist, w_in, b_in, dw_w, dw_b, ln_g, ln_b, w_out, b_out):
    z = a @ w_in + b_in
    g = z[..., :D_MODEL] * jax.nn.sigmoid(z[..., D_MODEL:])
    zc = jnp.concatenate([hist.astype(g.dtype), g], axis=1)
    y = lax.conv_general_dilated(zc, dw_w[:, None, :].astype(zc.dtype), (1,), 'VALID',
                                 dimension_numbers=('NWC', 'WIO', 'NWC'),
                                 feature_group_count=D_MODEL) + dw_b
    y = jax.nn.silu(layer_norm(y, ln_g, ln_b))
    out = y @ w_out + b_out
    return out, zc[:, -(CONV_WIDTH - 1):]


def moba_attention(q, k, v, q_pos, rel_bias):
    n, sq, h, hd = q.shape
    l = k.shape[1]
    lp = -(-l // MOBA_BLOCK) * MOBA_BLOCK
    n_blk = lp // MOBA_BLOCK
    pad = ((0, 0), (0, lp - l), (0, 0), (0, 0))
    kb = jnp.pad(k, pad).reshape(n, n_blk, MOBA_BLOCK, h, hd)
    vb = jnp.pad(v, pad).reshape(n, n_blk, MOBA_BLOCK, h, hd)
    k_mean = kb.astype(jnp.float32).mean(axis=2)
    k_sel = min(MOBA_TOPK, n_blk)
    qc = math.gcd(sq, MOBA_Q_CHUNK)
    n_chunk = sq // qc
    q_items = q.reshape(n * n_chunk, qc, h, hd)
    pos_items = jnp.broadcast_to(q_pos.reshape(1, n_chunk, qc), (n, n_chunk, qc)).reshape(n * n_chunk, qc)
    seq_items = jnp.repeat(jnp.arange(n, dtype=jnp.int32), n_chunk)
    head_idx = jnp.arange(h)
    blk_ids = jnp.arange(n_blk)
    key_off = jnp.arange(MOBA_BLOCK, dtype=jnp.int32)

    def chunk(args):
        qch, pos, s = args
        kseq, vseq, mseq = kb[s], vb[s], k_mean[s]
        qf = qch.astype(jnp.float32) * (hd ** -0.5)
        own = pos // MOBA_BLOCK
        gate = jnp.einsum('qhd,bhd->qhb', qf, mseq)
        gate = jnp.where(blk_ids[None, None, :] < own[:, None, None], gate, -jnp.inf)
        _, sel = lax.top_k(gate, k_sel)
        valid = sel < own[:, None, None]
        own_b = jnp.broadcast_to(own[:, None, None], (qc, h, 1)).astype(sel.dtype)
        blocks = jnp.concatenate([sel, own_b], axis=-1)
        ok = jnp.concatenate([valid, jnp.ones(own_b.shape, bool)], axis=-1)
        k_g = kseq[blocks, :, head_idx[None, :, None]]
        v_g = vseq[blocks, :, head_idx[None, :, None]]
        logits = jnp.einsum('qhd,qhbjd->qhbj', qf, k_g.astype(jnp.float32))
        key_pos = blocks[..., None] * MOBA_BLOCK + key_off
        dist = pos[:, None, None, None] - key_pos
        bias = rel_bias[t5_bucket(jnp.maximum(dist, 0)), head_idx[None, :, None, None]].astype(jnp.float32)
        mask = ok[..., None] & (dist >= 0)
        logits = jnp.where(mask, logits + bias, -jnp.inf)
        probs = jax.nn.softmax(logits.reshape(qc, h, -1), axis=-1).reshape(logits.shape)
        out = jnp.einsum('qhbj,qhbjd->qhd', probs, v_g.astype(jnp.float32))
        return out.astype(v.dtype)

    out = lax.map(chunk, (q_items, pos_items, seq_items))
    return out.reshape(n, sq, h, hd)


def moba_mixer(a, k_past, v_past, q_pos, w_qkv, w_o, rel_bias):
    n, s, _ = a.shape
    qkv = (a @ w_qkv).reshape(n, s, 3, N_HEADS, HEAD_DIM)
    q, k, v = qkv[:, :, 0], qkv[:, :, 1], qkv[:, :, 2]
    if k_past is None:
        kf, vf = k, v
    else:
        kf = jnp.concatenate([k_past.astype(k.dtype), k], axis=1)
        vf = jnp.concatenate([v_past.astype(v.dtype), v], axis=1)
    o = moba_attention(q, kf, vf, q_pos, rel_bias)
    return o.reshape(n, s, N_HEADS * HEAD_DIM) @ w_o, k, v


def peer_ffn(x, w_q, sub_keys, u, v):
    shp = x.shape
    xt = x.reshape(-1, shp[-1])
    t = xt.shape[0]
    q = (xt @ w_q).astype(jnp.float32).reshape(t, PEER_HEADS, 2, PEER_KEY_HALF)
    s = jnp.einsum('thcd,hcnd->thcn', q, sub_keys.astype(jnp.float32))
    s1, i1 = lax.top_k(s[:, :, 0], PEER_TOPK)
    s2, i2 = lax.top_k(s[:, :, 1], PEER_TOPK)
    cand_s = (s1[..., :, None] + s2[..., None, :]).reshape(t, PEER_HEADS, -1)
    cand_i = (i1[..., :, None] * PEER_N_KEYS + i2[..., None, :]).reshape(t, PEER_HEADS, -1)
    top_s, top_pos = lax.top_k(cand_s, PEER_TOPK)
    e_idx = jnp.take_along_axis(cand_i, top_pos, axis=-1)
    g = jax.nn.softmax(top_s, axis=-1)
    c = min(PEER_CHUNK, t)
    tp = -(-t // c) * c
    xt_p = jnp.pad(xt, ((0, tp - t), (0, 0))).reshape(tp // c, c, shp[-1])
    e_p = jnp.pad(e_idx, ((0, tp - t), (0, 0), (0, 0))).reshape(tp // c, c, PEER_HEADS, PEER_TOPK)
    g_p = jnp.pad(g, ((0, tp - t), (0, 0), (0, 0))).reshape(tp // c, c, PEER_HEADS, PEER_TOPK)

    def chunk(args):
        xc, ec, gc = args
        act = jax.nn.gelu(jnp.einsum('cd,chkd->chk', xc, u[ec]).astype(jnp.float32), approximate=False)
        w = (gc * act).astype(xc.dtype)
        return jnp.einsum('chk,chkd->cd', w, v[ec])

    out = lax.map(chunk, (xt_p, e_p, g_p)).reshape(tp, shp[-1])[:t]
    return out.reshape(shp)


def ple_add(h, p, w_proj, w_gate, g_norm):
    gate = jax.nn.sigmoid((rms_norm(h, g_norm) @ w_gate).astype(jnp.float32))
    return h + (gate * (p @ w_proj).astype(jnp.float32)).astype(h.dtype)


def setup_inputs(seed: int = 0) -> dict:
    key = jax.random.key(seed)
    ks = jax.random.split(key, 40)
    f32 = jnp.float32

    def nrm(k, shape, scale):
        return jax.random.normal(k, shape, f32) * scale

    n_pages = PAST_LEN // PAGE_SIZE
    n_used = DEC_BATCH * n_pages
    n_phys = (5 * n_used + 3) // 4
    page_table = jax.random.permutation(ks[0], n_phys)[:n_used].reshape(DEC_BATCH, n_pages).astype(jnp.int32)
    D = D_MODEL
    return {
        'x_prompt': nrm(ks[1], (BATCH, SEQ, D), 1.0),
        'x_sample': nrm(ks[2], (DEC_BATCH, DEC_SEQ, D), 1.0),
        'state_conv': nrm(ks[3], (N_CONV_LAYERS, DEC_BATCH, CONV_WIDTH - 1, D), 0.5),
        'cache_k': nrm(ks[4], (N_ATTN_LAYERS, n_phys, PAGE_SIZE, N_HEADS, HEAD_DIM), 1.0),
        'cache_v': nrm(ks[5], (N_ATTN_LAYERS, n_phys, PAGE_SIZE, N_HEADS, HEAD_DIM), 1.0),
        'page_table': page_table,
        'p_prompt': nrm(ks[6], (DEPTH, BATCH, SEQ, PLE_DIM), 1.0),
        'p_sample': nrm(ks[7], (DEPTH, DEC_BATCH, DEC_SEQ, PLE_DIM), 1.0),
        'rel_bias': nrm(ks[8], (T5_BUCKETS, N_HEADS), 0.1),
        'norm_mix_g': 1.0 + nrm(ks[9], (DEPTH, D), 0.02),
        'norm_ffn_g': 1.0 + nrm(ks[10], (DEPTH, D), 0.02),
        'norm_ple_g': 1.0 + nrm(ks[11], (DEPTH, D), 0.02),
        'norm_final_g': 1.0 + nrm(ks[12], (D,), 0.02),
        'conv_w_in': nrm(ks[13], (N_CONV_LAYERS, D, 2 * D), D ** -0.5),
        'conv_b_in': nrm(ks[14], (N_CONV_LAYERS, 2 * D), 0.02),
        'conv_dw_w': nrm(ks[15], (N_CONV_LAYERS, CONV_WIDTH, D), CONV_WIDTH ** -0.5),
        'conv_dw_b': nrm(ks[16], (N_CONV_LAYERS, D), 0.02),
        'conv_ln_g': 1.0 + nrm(ks[17], (N_CONV_LAYERS, D), 0.02),
        'conv_ln_b': nrm(ks[18], (N_CONV_LAYERS, D), 0.02),
        'conv_w_out': nrm(ks[19], (N_CONV_LAYERS, D, D), D ** -0.5),
        'conv_b_out': nrm(ks[20], (N_CONV_LAYERS, D), 0.02),
        'attn_w_qkv': nrm(ks[21], (N_ATTN_LAYERS, D, 3 * N_HEADS * HEAD_DIM), D ** -0.5),
        'attn_w_o': nrm(ks[22], (N_ATTN_LAYERS, N_HEADS * HEAD_DIM, D), (N_HEADS * HEAD_DIM) ** -0.5),
        'peer_w_q': nrm(ks[23], (DEPTH, D, PEER_HEADS * 2 * PEER_KEY_HALF), D ** -0.5),
        'peer_sub_keys': nrm(ks[24], (DEPTH, PEER_HEADS, 2, PEER_N_KEYS, PEER_KEY_HALF), PEER_KEY_HALF ** -0.5),
        'peer_u': nrm(ks[25], (DEPTH, PEER_N_EXPERTS, D), D ** -0.5),
        'peer_v': nrm(ks[26], (DEPTH, PEER_N_EXPERTS, D), (PEER_HEADS * PEER_TOPK) ** -0.5),
        'ple_w_proj': nrm(ks[27], (DEPTH, PLE_DIM, D), PLE_DIM ** -0.5),
        'ple_w_gate': nrm(ks[28], (DEPTH, D, D), D ** -0.5),
    }


def reference(x_prompt, x_sample, state_conv, cache_k, cache_v, page_table, p_prompt, p_sample,
              rel_bias, norm_mix_g, norm_ffn_g, norm_ple_g, norm_final_g,
              conv_w_in, conv_b_in, conv_dw_w, conv_dw_b, conv_ln_g, conv_ln_b, conv_w_out, conv_b_out,
              attn_w_qkv, attn_w_o, peer_w_q, peer_sub_keys, peer_u, peer_v, ple_w_proj, ple_w_gate):
    n_b, seq = x_prompt.shape[0], x_prompt.shape[1]
    n_d, dec_seq = x_sample.shape[0], x_sample.shape[1]
    n_pages = page_table.shape[1]
    past_len = n_pages * PAGE_SIZE
    pos_p = jnp.arange(seq, dtype=jnp.int32)
    pos_s = past_len + jnp.arange(dec_seq, dtype=jnp.int32)
    hp, hs = x_prompt, x_sample
    conv_p, conv_s, kp, vp, ks_, vs_ = [], [], [], [], [], []
    for i in range(DEPTH):
        ap = rms_norm(hp, norm_mix_g[i])
        asm = rms_norm(hs, norm_mix_g[i])
        j = i // N_MIXERS
        if i % N_MIXERS == 0:
            cw = (conv_w_in[j], conv_b_in[j], conv_dw_w[j], conv_dw_b[j], conv_ln_g[j], conv_ln_b[j],
                  conv_w_out[j], conv_b_out[j])
            hist0 = jnp.zeros((n_b, CONV_WIDTH - 1, D_MODEL), hp.dtype)
            mp, st_p = conv_module(ap, hist0, *cw)
            ms, st_s = conv_module(asm, state_conv[j], *cw)
            conv_p.append(st_p)
            conv_s.append(st_s)
        else:
            k_past = cache_k[j][page_table].reshape(n_d, past_len, N_HEADS, HEAD_DIM)
            v_past = cache_v[j][page_table].reshape(n_d, past_len, N_HEADS, HEAD_DIM)
            mp, k_new_p, v_new_p = moba_mixer(ap, None, None, pos_p, attn_w_qkv[j], attn_w_o[j], rel_bias)
            ms, k_new_s, v_new_s = moba_mixer(asm, k_past, v_past, pos_s, attn_w_qkv[j], attn_w_o[j], rel_bias)
            kp.append(k_new_p)
            vp.append(v_new_p)
            ks_.append(k_new_s)
            vs_.append(v_new_s)
        hp = hp + mp
        hs = hs + ms
        hp = hp + peer_ffn(rms_norm(hp, norm_ffn_g[i]), peer_w_q[i], peer_sub_keys[i], peer_u[i], peer_v[i])
        hs = hs + peer_ffn(rms_norm(hs, norm_ffn_g[i]), peer_w_q[i], peer_sub_keys[i], peer_u[i], peer_v[i])
        hp = ple_add(hp, p_prompt[i], ple_w_proj[i], ple_w_gate[i], norm_ple_g[i])
        hs = ple_add(hs, p_sample[i], ple_w_proj[i], ple_w_gate[i], norm_ple_g[i])
    y_prompt = rms_norm(hp, norm_final_g)
    y_sample = rms_norm(hs, norm_final_g)
    new_state_conv_prompt = jnp.stack(conv_p)
    new_state_conv_sample = jnp.stack(conv_s)
    new_cache_k_prompt = jnp.stack(kp)
    new_cache_v_prompt = jnp.stack(vp)
    new_cache_k_sample = jnp.stack(ks_)
    new_cache_v_sample = jnp.stack(vs_)
    return (y_prompt, y_sample, new_state_conv_prompt, new_state_conv_sample,
            new_cache_k_prompt, new_cache_v_prompt, new_cache_k_sample, new_cache_v_sample)
```

```python
import contextlib
import numpy as np
import concourse.bass as bass
import concourse.mybir as mybir
from concourse.bass_utils import run_bass_kernel_spmd

F32 = mybir.dt.float32
BF16 = mybir.dt.bfloat16
I32 = mybir.dt.int32
U32 = mybir.dt.uint32
AF = mybir.ActivationFunctionType
ALU = mybir.AluOpType
AX = mybir.AxisListType

ENGS = ['pe', 'act', 'dve', 'pool', 'sp']
N_DMA_SEM = 8


class Buf:
    __slots__ = ('name', 'last_w', 'readers')

    def __init__(self, name='b'):
        self.name = name
        self.last_w = None
        self.readers = []


class _Rec:
    def __init__(self):
        self.call = None

    def __getattr__(self, name):
        def f(*a, **k):
            self.call = (name, a, k)
            return self
        return f


def _record(fn):
    r = _Rec()
    fn(r)
    assert r.call is not None
    return r.call


class Sched:
    def __init__(self, nc, stack):
        self.nc = nc
        self.ops = {e: [] for e in ENGS}
        self.cnt = {e: 0 for e in ENGS}
        self.seen = {e: {} for e in ENGS}
        self.esem = {e: stack.enter_context(nc.semaphore('es_' + e)) for e in ENGS}
        self.dsem = {q: [stack.enter_context(nc.semaphore('ds_%s%d' % (q, i))) for i in range(N_DMA_SEM)]
                     for q in ('sp', 'pool', 'act')}
        self.dcnt = {q: 0 for q in ('sp', 'pool', 'act')}
        self.semobj = {}
        for e in ENGS:
            self.semobj[('e', e)] = self.esem[e]
        for q in self.dsem:
            for i, s in enumerate(self.dsem[q]):
                self.semobj[('d', q, i)] = s
        self.final_tokens = []

    def _collect(self, eng, reads, writes, extra=()):
        need = {}

        def add(tok):
            if tok is None:
                return
            k, v = tok
            if v > need.get(k, 0):
                need[k] = v
        for b in reads:
            add(b.last_w)
        for b in writes:
            add(b.last_w)
            for r in b.readers:
                add(r)
        for t in extra:
            add(t)
        waits = []
        seen = self.seen[eng]
        for k, v in need.items():
            if eng == 'pe' and k == ('e', 'pe'):
                continue
            if v > seen.get(k, 0):
                waits.append((k, v))
                seen[k] = v
        return waits

    def _post(self, tok, reads, writes):
        for b in reads:
            if len(b.readers) > 64:
                mx = {}
                for k, v in b.readers:
                    if v > mx.get(k, 0):
                        mx[k] = v
                b.readers = list(mx.items())
            b.readers.append(tok)
        for b in writes:
            b.last_w = tok
            b.readers = []

    def op(self, eng, fn, reads=(), writes=()):
        waits = self._collect(eng, reads, writes)
        idx = self.cnt[eng]
        self.cnt[eng] += 1
        tok = (('e', eng), idx + 1)
        self.ops[eng].append((waits, _record(fn), (self.esem[eng], 1)))
        self._post(tok, reads, writes)
        return tok

    def dma(self, q, fn, reads=(), writes=(), final=False):
        k = self.dcnt[q]
        self.dcnt[q] += 1
        slot = k % N_DMA_SEM
        rnd = k // N_DMA_SEM
        key = ('d', q, slot)
        extra = [(key, 16 * rnd)] if rnd > 0 else []
        waits = self._collect(q, reads, writes, extra)
        tok = (key, 16 * (rnd + 1))
        self.ops[q].append((waits, _record(fn), (self.dsem[q][slot], 16)))
        self._post(tok, reads, writes)
        if final:
            self.final_tokens.append(tok)
        return tok

    def merge(self, olds, name='m'):
        nb = Buf(name)
        for o in olds:
            if o.last_w is not None:
                nb.readers.append(o.last_w)
            nb.readers.extend(o.readers)
        return nb

    def emit(self):
        nc = self.nc
        fin = {}
        for k, v in self.final_tokens:
            fin[k] = max(fin.get(k, 0), v)
        for e in ENGS:
            if e != 'sp' and self.cnt[e] > 0:
                fin[('e', e)] = self.cnt[e]
        for q in self.dsem:
            kk = self.dcnt[q]
            for slot in range(N_DMA_SEM):
                n = (kk - slot + N_DMA_SEM - 1) // N_DMA_SEM if kk > slot else 0
                if n > 0:
                    fin[('d', q, slot)] = max(fin.get(('d', q, slot), 0), 16 * n)
        final_waits = list(fin.items())
        semobj = self.semobj
        ops = self.ops

        def replay(eobj, lst, extra_waits=()):
            for waits, fn, inc in lst:
                for k, v in waits:
                    eobj.wait_ge(semobj[k], v)
                ins = getattr(eobj, fn[0])(*fn[1], **fn[2])
                ins.then_inc(inc[0], inc[1])
            for k, v in extra_waits:
                eobj.wait_ge(semobj[k], v)

        with nc.Block() as block:
            @block.sync
            def _(e):
                replay(e, ops['sp'], final_waits)

            @block.tensor
            def _(e):
                replay(e, ops['pe'])

            @block.scalar
            def _(e):
                replay(e, ops['act'])

            @block.vector
            def _(e):
                replay(e, ops['dve'])

            @block.gpsimd
            def _(e):
                replay(e, ops['pool'])


def mkap(base, off, dims):
    p = base.ap[0]
    return bass.AP(tensor=base.tensor, offset=base.offset + off, ap=[[p[0], p[1]]] + [list(d) for d in dims])


D = 1024
KC = 8
NEXP = 16384
NCH = 128
PH = 8
TT = 256


NP_OWN = 2048
NS = 16
NCOL = NP_OWN + NS
ARENA_F32 = 17000


class Arena:
    def __init__(self, kb, nwords):
        self.kb = kb
        self.t = kb.sb('arena', [128, nwords], F32)
        self.n = nwords
        self.off = 0
        self.cur = []
        self.prev = []

    def phase(self):
        allb = self.cur + self.prev
        self.prev = [self.kb.S.merge(allb, 'arena_prev')] if allb else []
        self.cur = []
        self.off = 0

    def alloc(self, shape, dt, name='a'):
        nfree = 1
        for d in shape[1:]:
            nfree *= d
        bpe = 4 if dt in (F32, I32, U32) else 2
        words = (nfree * bpe + 3) // 4
        words = (words + 7) // 8 * 8
        assert self.off + words <= self.n, ('arena overflow', name, self.off, words, self.n)
        base = self.t[:, self.off:self.off + words]
        self.off += words
        if bpe == 2:
            ap = base.bitcast(dt)[:, 0:nfree]
        elif dt != F32:
            ap = base.bitcast(dt)[:, 0:nfree]
        else:
            ap = base[:, 0:nfree]
        if len(shape) > 2:
            names = ' '.join('d%d' % i for i in range(len(shape) - 1))
            kw = {'d%d' % i: shape[i + 1] for i in range(len(shape) - 1)}
            ap = ap.rearrange('p (%s) -> p %s' % (names, names), **kw)
        b = self.kb.S.merge(self.prev, name) if self.prev else Buf(name)
        self.cur.append(b)
        return ap, b


class KB:
    def __init__(self, nc, st):
        self.nc = nc
        self.st = st
        self.S = Sched(nc, st)
        S = self.S
        self.ps = [st.enter_context(nc.psum_tensor('psb%d' % i, [128, 512], F32)) for i in range(8)]
        self.b_ps = [Buf('ps%d' % i) for i in range(8)]
        self.ident_f = self.sb('ident_f', [128, 128], F32)
        self.ident_b = self.sb('ident_b', [128, 128], BF16)
        self.iota_f = self.sb('iota_f', [128, 128], F32)
        self.iota_p = self.sb('iota_p', [128, 1], F32)
        self.ones_f = self.sb('ones_f', [128, 128], F32)
        self.ones_b = self.sb('ones_b', [128, 128], BF16)
        self.b_const = Buf('const')
        tmp_i = self.sb('tmp_iota_i', [128, 128], I32)
        tmp_f = self.sb('tmp_iota_f', [128, 128], F32)
        bt = Buf('tmpi')
        S.op('pool', lambda e: e.iota(tmp_i[:], pattern=[[1, 128]], base=0, channel_multiplier=0), writes=[bt])
        S.op('dve', lambda e: e.tensor_copy(out=self.iota_f[:], in_=tmp_i[:]), reads=[bt], writes=[self.b_const])
        S.op('pool', lambda e: e.iota(tmp_i[:], pattern=[[1, 128]], base=0, channel_multiplier=-1), reads=[], writes=[bt])
        bt2 = Buf('tmpf')
        S.op('dve', lambda e: e.tensor_copy(out=tmp_f[:], in_=tmp_i[:]), reads=[bt], writes=[bt2])
        S.op('dve', lambda e: e.tensor_single_scalar(out=self.ident_f[:], in_=tmp_f[:], scalar=0.0, op=ALU.is_equal),
             reads=[bt2], writes=[self.b_const])
        S.op('dve', lambda e: e.tensor_copy(out=self.ident_b[:], in_=self.ident_f[:]), reads=[self.b_const], writes=[self.b_const])
        S.op('pool', lambda e: e.memset(self.ones_f[:], 1.0), writes=[self.b_const])
        S.op('pool', lambda e: e.memset(self.ones_b[:], 1.0), writes=[self.b_const])
        S.op('pool', lambda e: e.iota(tmp_i[:, 0:1], pattern=[[1, 1]], base=0, channel_multiplier=1), reads=[], writes=[bt])
        S.op('dve', lambda e: e.tensor_copy(out=self.iota_p[:], in_=tmp_i[:, 0:1]), reads=[bt], writes=[self.b_const])
        self.n_xtok = 0
        self.rr = 0

    def set_xtok(self, ar):
        self.xtok = []
        self.b_xtok = []
        for i in range(2):
            a, b = ar.alloc([128, 1024], F32, 'xtok')
            self.xtok.append(a)
            self.b_xtok.append(b)

    def sb(self, name, shape, dt):
        return self.st.enter_context(self.nc.sbuf_tensor(name, shape, dt))

    def bank(self):
        b = self.rr % 5
        self.rr += 1
        return b

    def load_T(self, x_rows, P, dst_fn, b_dst, ncols=1024, q='sp', b_src=None):
        S = self.S
        k = self.n_xtok % 2
        self.n_xtok += 1
        xt = self.xtok[k]
        S.dma(q, lambda e: e.dma_start(out=xt[0:P, 0:ncols], in_=x_rows), reads=([b_src] if b_src is not None else []), writes=[self.b_xtok[k]])
        nkc = ncols // 128
        for g0 in range(0, nkc, 4):
            nk = min(4, nkc - g0)
            pbi = 5 + ((g0 // 4) % 2)
            pb = self.ps[pbi]
            for j in range(nk):
                kc = g0 + j
                S.op('pe', lambda e: e.transpose(out=pb[:, j * 128:j * 128 + P], in_=xt[0:P, kc * 128:(kc + 1) * 128],
                                                 identity=self.ident_f[0:P, 0:P]),
                     reads=[self.b_xtok[k], self.b_const], writes=[self.b_ps[pbi]])
            src = pb[:, 0:nk * 128].rearrange('p (j t) -> p j t', j=nk)[:, :, 0:P]
            S.op('act', lambda e: e.activation(out=dst_fn(g0, nk), in_=src, func=AF.Copy),
                 reads=[self.b_ps[pbi]], writes=[b_dst])

    def store_T(self, src_fn, b_src, P, out_rows, b_out, final=True, q='sp'):
        S = self.S
        k = self.n_xtok % 2
        self.n_xtok += 1
        xt = self.xtok[k]
        for g0 in range(0, KC, 4):
            pbi = 5 + ((g0 // 4) % 2)
            pb = self.ps[pbi]
            for j in range(4):
                kc = g0 + j
                S.op('pe', lambda e: e.transpose(out=pb[0:P, j * 128:(j + 1) * 128], in_=src_fn(kc), identity=self.ident_f[:]),
                     reads=[b_src, self.b_const], writes=[self.b_ps[pbi]])
            S.op('act', lambda e: e.activation(out=xt[0:P, g0 * 128:(g0 + 4) * 128], in_=pb[0:P, :], func=AF.Copy),
                 reads=[self.b_ps[pbi]], writes=[self.b_xtok[k]])
        S.dma(q, lambda e: e.dma_start(out=out_rows, in_=xt[0:P, :]), reads=[self.b_xtok[k]], writes=[b_out], final=final)

    def rmsnorm_to(self, ar, hT, b_hT, c0, W, gvec, b_gvec, out_ap, b_out, tmp=None):
        S = self.S
        if tmp is None:
            sq, b_sq = ar.alloc([128, KC, TT], F32, 'sq')
            rstd, b_rstd = ar.alloc([128, TT], F32, 'rstd')
        else:
            sq, b_sq, rstd, b_rstd = tmp
        S.op('act', lambda e: e.activation(out=sq[:, :, 0:W], in_=hT[:, :, c0:c0 + W], func=AF.Square), reads=[b_hT], writes=[b_sq])
        pb = self.ps[7]
        for kc in range(KC):
            S.op('pe', lambda e: e.matmul(pb[:, 0:W], lhsT=self.ones_f[:], rhs=sq[:, kc, 0:W], start=(kc == 0), stop=(kc == KC - 1)),
                 reads=[b_sq, self.b_const], writes=[self.b_ps[7]])
        S.op('act', lambda e: e.activation(out=rstd[:, 0:W], in_=pb[:, 0:W], func=AF.Sqrt, scale=1.0 / D, bias=1e-6),
             reads=[self.b_ps[7]], writes=[b_rstd])
        S.op('dve', lambda e: e.reciprocal(out=rstd[:, 0:W], in_=rstd[:, 0:W]), reads=[b_rstd], writes=[b_rstd])
        for kc in range(KC):
            S.op('dve', lambda e: e.scalar_tensor_tensor(out=out_ap[:, kc, 0:W], in0=hT[:, kc, c0:c0 + W], scalar=gvec[:, kc:kc + 1],
                                                         in1=rstd[:, 0:W], op0=ALU.mult, op1=ALU.mult),
                 reads=[b_hT, b_rstd, b_gvec], writes=[b_out])

    def lin(self, ps_ap, b_psb, Wsb, b_W, n0, nn, xin, b_x, W, nkc=KC):
        S = self.S
        for kc in range(nkc):
            S.op('pe', lambda e: e.matmul(ps_ap, lhsT=Wsb[:, kc, n0:n0 + nn], rhs=xin[:, kc, 0:W], start=(kc == 0), stop=(kc == nkc - 1)),
                 reads=[b_W, b_x], writes=[b_psb])

    def load_w(self, dst, b_dst, w_dram, nkc=KC):
        self.S.dma('pool', lambda e: e.dma_start(out=dst, in_=w_dram.rearrange('(kc p) n -> p kc n', p=128)), writes=[b_dst])


class Peer:
    def __init__(self, kb, R):
        self.kb = kb
        self.Gt = R
        self.b_Gt = Buf('Gt')
        self.cnt_ab = 0
        self.b_ps4 = [Buf('ps4a'), Buf('ps4b')]

    def alloc(self, ar):
        a = ar.alloc
        self.wq, self.b_wq = a([128, KC, 1024], BF16, 'wq')
        self.skT, self.b_skT = a([128, PH, 2, 128], BF16, 'skT')
        self.xnT, self.b_xnT = a([128, KC, TT], BF16, 'xnT')
        self.scr8, self.b_scr8 = a([128, 2048], F32, 'scr8')
        self.sq = self.scr8.rearrange('p (k t) -> p k t', k=KC); self.b_sq = self.b_scr8
        self.rstd, self.b_rstd = a([128, TT], F32, 'rstd')
        self.qT, self.b_qT = a([128, PH, TT], BF16, 'qT')
        self.stop, self.b_stop = a([128, PH, 2, 16], F32, 'stop')
        self.itop, self.b_itop = a([128, PH, 2, 16], U32, 'itop')
        self.itopf, self.b_itopf = a([128, PH, 2, 16], F32, 'itopf')
        self.smod, self.b_smod = a([128, 256], F32, 'smod')
        self.cand, self.b_cand = a([128, 256], F32, 'cand')
        self.ctop, self.b_ctop = a([128, PH, 16], F32, 'ctop')
        self.cpos, self.b_cpos = a([128, PH, 16], U32, 'cpos')
        self.ak, self.b_ab = a([128, 128], U32, 'ak')
        self.bk, _ = a([128, 128], U32, 'bk')
        self.akf, _ = a([128, 128], F32, 'akf')
        self.bkf, _ = a([128, 128], F32, 'bkf')
        self.negmax, self.b_negmax = a([128, PH], F32, 'negmax')
        self.Z, self.b_Z = a([128, PH], F32, 'Z')
        self.ee, self.b_ee = a([128, PH, 16], F32, 'ee')
        self.g, self.b_g = a([128, 128], F32, 'g')
        self.i1s, self.b_i1s = a([128, 128], F32, 'i1s')
        self.i2s, self.b_i2s = a([128, 128], F32, 'i2s')
        self.gT, self.b_gT = a([128, TT], F32, 'gT')
        self.i1T, self.b_i1T = a([128, TT], F32, 'i1T')
        self.i2T, self.b_i2T = a([128, TT], F32, 'i2T')
        self.NAB = 2
        self.A, self.b_A, self.B, self.b_B = [], [], [], []
        for i in range(self.NAB):
            x, b = a([128, 128], BF16, 'A'); self.A.append(x); self.b_A.append(b)
            x, b = a([128, 128], BF16, 'B'); self.B.append(x); self.b_B.append(b)
        self.uTb, self.b_uTb, self.vb, self.b_vb, self.ge, self.b_ge, self.wT, self.b_wT = [], [], [], [], [], [], [], []
        for i in range(2):
            x, b = a([128, KC, 128], BF16, 'uTb'); self.uTb.append(x); self.b_uTb.append(b)
            x, b = a([128, 1024], BF16, 'vb'); self.vb.append(x); self.b_vb.append(b)
            x, b = a([128, TT], BF16, 'ge'); self.ge.append(x); self.b_ge.append(b)
            x, b = a([128, TT], BF16, 'wT'); self.wT.append(x); self.b_wT.append(b)
        self.iota16, b = a([128, 16], F32, 'iota16')
        kb = self.kb
        kb.S.op('dve', lambda e: e.tensor_copy(out=self.iota16[:], in_=kb.iota_f[:, 0:16]), reads=[kb.b_const], writes=[b])
        self.b_iota16 = b

    def prepass(self, ar, u_l, v_l, uT_scr, v_scr, b_scr):
        kb = self.kb; S = kb.S
        ps = kb.ps
        unat, b_unat, uTs, b_uTs = [], [], [], []
        for i in range(2):
            x, b = ar.alloc([128, 1024], BF16, 'unat'); unat.append(x); b_unat.append(b)
            x, b = ar.alloc([128, 1024], BF16, 'uTs'); uTs.append(x); b_uTs.append(b)
        for i in range(NCH):
            k = i % 2
            un, ut = unat[k], uTs[k]
            S.dma('pool', lambda e: e.dma_start(out=un, in_=u_l[i * 128:(i + 1) * 128, :]), writes=[b_unat[k]])
            pb = ps[5 + k]
            psv = pb[:].bitcast(BF16)
            for kc in range(KC):
                S.op('pe', lambda e: e.transpose(out=psv[:, kc * 128:(kc + 1) * 128], in_=un[:, kc * 128:(kc + 1) * 128], identity=kb.ident_b[:]),
                     reads=[b_unat[k], kb.b_const], writes=[kb.b_ps[5 + k]])
            if k == 0:
                S.op('act', lambda e: e.activation(out=ut, in_=psv, func=AF.Copy), reads=[kb.b_ps[5 + k]], writes=[b_uTs[k]])
            else:
                S.op('dve', lambda e: e.tensor_copy(out=ut, in_=psv), reads=[kb.b_ps[5 + k]], writes=[b_uTs[k]])
            S.dma('sp', lambda e: e.dma_start(out=uT_scr[i], in_=ut), reads=[b_uTs[k]], writes=[b_scr])
        for i in range(NCH):
            k = i % 2
            un = unat[k]
            S.dma('pool', lambda e: e.dma_start(out=un, in_=v_l[i * 128:(i + 1) * 128, :]), writes=[b_unat[k]])
            S.dma('sp', lambda e: e.dma_start(out=v_scr[i * 128:(i + 1) * 128, :], in_=un), reads=[b_unat[k]], writes=[b_scr])

    def load_layer(self, wq_l, sk_l):
        kb = self.kb; S = kb.S
        kb.load_w(self.wq, self.b_wq, wq_l)
        sknat = self.scr8[:, 0:1024].rearrange('p (h c) -> p h c', h=PH)
        for c in range(2):
            S.dma('sp', lambda e: e.dma_start(out=sknat[:, :, c * 64:(c + 1) * 64], in_=sk_l[:, c].rearrange('h n d -> n h d')),
                  writes=[self.b_scr8])
        S.op('pool', lambda e: e.memset(self.skT, 0.0), writes=[self.b_skT])
        for h in range(PH):
            pb = kb.ps[7]
            S.op('pe', lambda e: e.transpose(out=pb[:, 0:128], in_=sknat[:, h, :], identity=kb.ident_f[:]),
                 reads=[self.b_scr8, kb.b_const], writes=[kb.b_ps[7]])
            S.op('act', lambda e: e.activation(out=self.skT[0:64, h, 0, :], in_=pb[0:64, 0:128], func=AF.Copy),
                 reads=[kb.b_ps[7]], writes=[self.b_skT])
            S.op('act', lambda e: e.activation(out=self.skT[64:128, h, 1, :], in_=pb[64:128, 0:128], func=AF.Copy),
                 reads=[kb.b_ps[7]], writes=[self.b_skT])

    def top16(self, src_ap, b_src, n, top_ap, idx_ap, b_top, b_idx):
        S = self.kb.S
        P = src_ap.ap[0][1]
        smv = self.smod[0:P, 0:n]
        S.op('dve', lambda e: e.max(out=top_ap[:, 0:8], in_=src_ap), reads=[b_src], writes=[b_top])
        S.op('dve', lambda e: e.max_index(out=idx_ap[:, 0:8], in_max=top_ap[:, 0:8], in_values=src_ap), reads=[b_src, b_top], writes=[b_idx])
        S.op('dve', lambda e: e.match_replace(out=smv, in_to_replace=top_ap[:, 0:8], in_values=src_ap, imm_value=-1e30),
             reads=[b_src, b_top], writes=[self.b_smod])
        S.op('dve', lambda e: e.max(out=top_ap[:, 8:16], in_=smv), reads=[self.b_smod], writes=[b_top])
        S.op('dve', lambda e: e.max_index(out=idx_ap[:, 8:16], in_max=top_ap[:, 8:16], in_values=smv), reads=[self.b_smod, b_top], writes=[b_idx])

    def route_tile(self, W):
        kb = self.kb; S = kb.S
        ps = kb.ps; b_ps = kb.b_ps
        for h in range(PH):
            pbi = 5 + (h % 2)
            pb = ps[pbi]
            kb.lin(pb[:, 0:W], b_ps[pbi], self.wq, self.b_wq, h * 128, 128, self.xnT, self.b_xnT, W)
            S.op('act', lambda e: e.activation(out=self.qT[:, h, 0:W], in_=pb[:, 0:W], func=AF.Copy), reads=[b_ps[pbi]], writes=[self.b_qT])
        nsub = (W + 127) // 128
        for sub in range(nsub):
            P = min(128, W - sub * 128)
            t0 = sub * 128
            for h in range(PH):
                pbi = 5 + (h % 2)
                pb = ps[pbi]
                S.op('pe', lambda e: e.matmul(pb[0:P, 0:256], lhsT=self.qT[:, h, t0:t0 + P],
                                              rhs=self.skT[:, h, :, :].rearrange('p c n -> p (c n)'), start=True, stop=True),
                     reads=[self.b_qT, self.b_skT], writes=[b_ps[pbi]])
                for c in range(2):
                    self.top16(pb[0:P, c * 128:(c + 1) * 128], b_ps[pbi], 128, self.stop[0:P, h, c, :], self.itop[0:P, h, c, :],
                               self.b_stop, self.b_itop)
            S.op('dve', lambda e: e.tensor_copy(out=self.itopf[0:P], in_=self.itop[0:P]), reads=[self.b_itop], writes=[self.b_itopf])
            for h in range(PH):
                base = self.stop[0:P, h, 0, :]
                S.op('dve', lambda e: e.tensor_tensor(out=self.cand[0:P, :].rearrange('p (a b) -> p a b', a=16),
                                                      in0=mkap(base, 0, [[1, 16], [0, 16]]), in1=mkap(base, 16, [[0, 16], [1, 16]]), op=ALU.add),
                     reads=[self.b_stop], writes=[self.b_cand])
                self.top16(self.cand[0:P, :], self.b_cand, 256, self.ctop[0:P, h, :], self.cpos[0:P, h, :], self.b_ctop, self.b_cpos)
            S.op('dve', lambda e: e.tensor_single_scalar(out=self.negmax[0:P], in_=self.ctop[0:P, :, 0], scalar=-1.0, op=ALU.mult),
                 reads=[self.b_ctop], writes=[self.b_negmax])
            for h in range(PH):
                S.op('act', lambda e: e.activation(out=self.ee[0:P, h, :], in_=self.ctop[0:P, h, :], func=AF.Exp,
                                                   bias=self.negmax[0:P, h:h + 1], accum_out=self.Z[0:P, h:h + 1]),
                     reads=[self.b_ctop, self.b_negmax], writes=[self.b_ee, self.b_Z])
            S.op('dve', lambda e: e.reciprocal(out=self.Z[0:P], in_=self.Z[0:P]), reads=[self.b_Z], writes=[self.b_Z])
            S.op('dve', lambda e: e.tensor_tensor(out=self.g[0:P].rearrange('p (h k) -> p h k', h=PH), in0=self.ee[0:P],
                                                  in1=mkap(self.Z[0:P], 0, [[1, PH], [0, 16]]), op=ALU.mult),
                 reads=[self.b_ee, self.b_Z], writes=[self.b_g])
            cposf = self.cpos[0:P].rearrange('p h k -> p (h k)')
            S.op('dve', lambda e: e.tensor_single_scalar(out=self.ak[0:P], in_=cposf, scalar=4, op=ALU.logical_shift_right),
                 reads=[self.b_cpos], writes=[self.b_ab])
            S.op('dve', lambda e: e.tensor_single_scalar(out=self.bk[0:P], in_=cposf, scalar=15, op=ALU.bitwise_and),
                 reads=[self.b_cpos], writes=[self.b_ab])
            S.op('dve', lambda e: e.tensor_copy(out=self.akf[0:P], in_=self.ak[0:P]), reads=[self.b_ab], writes=[self.b_ab])
            S.op('dve', lambda e: e.tensor_copy(out=self.bkf[0:P], in_=self.bk[0:P]), reads=[self.b_ab], writes=[self.b_ab])
            HH = 4
            oh = self.scr8[0:P, 0:1024]
            oh2 = self.scr8[0:P, 1024:2048]
            for (kf, cc, outs, b_outs) in ((self.akf, 0, self.i1s, self.b_i1s), (self.bkf, 1, self.i2s, self.b_i2s)):
                for hh in range(0, PH, HH):
                    S.op('dve', lambda e: e.tensor_tensor(out=oh.rearrange('p (m a) -> p m a', a=16),
                                                          in0=mkap(kf[0:P, hh * 16:hh * 16 + 1], 0, [[1, HH * 16], [0, 16]]),
                                                          in1=mkap(self.iota16[0:P], 0, [[0, HH * 16], [1, 16]]), op=ALU.is_equal),
                         reads=[self.b_ab, self.b_iota16], writes=[self.b_scr8])
                    itb = self.itopf[0:P, hh, cc, :]
                    S.op('pool', lambda e: e.tensor_tensor(out=oh2.rearrange('p (h k a) -> p h k a', h=HH, k=16),
                                                           in0=oh.rearrange('p (h k a) -> p h k a', h=HH, k=16),
                                                           in1=mkap(itb, 0, [[32, HH], [0, 16], [1, 16]]), op=ALU.mult),
                         reads=[self.b_scr8, self.b_itopf], writes=[self.b_scr8])
                    S.op('dve', lambda e: e.tensor_reduce(out=outs[0:P, hh * 16:(hh + HH) * 16], in_=oh2.rearrange('p (m a) -> p m a', a=16),
                                                          axis=AX.X, op=ALU.add),
                         reads=[self.b_scr8], writes=[b_outs])
            for (src, b_src, dst, b_dst) in ((self.g, self.b_g, self.gT, self.b_gT), (self.i1s, self.b_i1s, self.i1T, self.b_i1T),
                                            (self.i2s, self.b_i2s, self.i2T, self.b_i2T)):
                pb = ps[7]
                S.op('pe', lambda e: e.transpose(out=pb[:, 0:P], in_=src[0:P, :], identity=kb.ident_f[0:P, 0:P]),
                     reads=[b_src, kb.b_const], writes=[b_ps[7]])
                S.op('act', lambda e: e.activation(out=dst[:, t0:t0 + P], in_=pb[:, 0:P], func=AF.Copy), reads=[b_ps[7]], writes=[b_dst])
        Gt = self.Gt
        GW = TT
        for t in range(W):
            k = self.cnt_ab % self.NAB
            self.cnt_ab += 1
            A, Bm = self.A[k], self.B[k]
            S.op('dve', lambda e: e.tensor_scalar(out=A, in0=kb.iota_f[:], scalar1=self.i1T[:, t:t + 1], scalar2=self.gT[:, t:t + 1],
                                                  op0=ALU.is_equal, op1=ALU.mult),
                 reads=[kb.b_const, self.b_i1T, self.b_gT], writes=[self.b_A[k]])
            S.op('pool', lambda e: e.tensor_scalar(out=Bm, in0=kb.iota_f[:], scalar1=self.i2T[:, t:t + 1], scalar2=None, op0=ALU.is_equal),
                 reads=[kb.b_const, self.b_i2T], writes=[self.b_B[k]])
            grp = t // 4
            pbi = 5 + (grp % 2)
            pb = ps[pbi]
            slot = t % 4
            S.op('pe', lambda e: e.matmul(pb[:, slot * 128:(slot + 1) * 128], lhsT=Bm, rhs=A, start=True, stop=True),
                 reads=[self.b_A[k], self.b_B[k]], writes=[b_ps[pbi]])
            if slot == 3 or t == W - 1:
                nt = slot + 1
                tb = t - slot
                outap = mkap(Gt[:, 0:1], tb, [[1, nt], [GW, 128]])
                S.op('act', lambda e: e.activation(out=outap, in_=pb[:, 0:nt * 128].rearrange('p (t i) -> p t i', t=nt), func=AF.Copy),
                     reads=[b_ps[pbi]], writes=[self.b_Gt])

    def expert_loop(self, W, uT_scr, v_scr, b_scr, hT, b_hT, c0):
        kb = self.kb; S = kb.S
        ps = kb.ps; b_ps = kb.b_ps
        GW = TT
        for i in range(NCH):
            k = i % 2
            S.dma('sp', lambda e: e.dma_start(out=self.uTb[k].rearrange('p kc e -> p (kc e)'), in_=uT_scr[i]), reads=[b_scr], writes=[self.b_uTb[k]])
            S.dma('sp', lambda e: e.dma_start(out=self.vb[k], in_=v_scr[i * 128:(i + 1) * 128, :]), reads=[b_scr], writes=[self.b_vb[k]])
            pav = ps[4][:, k * 256:k * 256 + W]
            b_pa = self.b_ps4[k]
            for kc in range(KC):
                S.op('pe', lambda e: e.matmul(pav, lhsT=self.uTb[k][:, kc, :], rhs=self.xnT[:, kc, 0:W], start=(kc == 0), stop=(kc == KC - 1)),
                     reads=[self.b_uTb[k], self.b_xnT], writes=[b_pa])
            S.op('act', lambda e: e.activation(out=self.ge[k][:, 0:W], in_=pav, func=AF.Gelu), reads=[b_pa], writes=[self.b_ge[k]])
            gsl = mkap(self.Gt[:, 0:1], i * GW, [[1, W]])
            S.op('dve', lambda e: e.tensor_tensor(out=self.wT[k][:, 0:W], in0=self.ge[k][:, 0:W], in1=gsl, op=ALU.mult),
                 reads=[self.b_ge[k], self.b_Gt], writes=[self.b_wT[k]])
            for dc in range(KC):
                po = ps[dc // 2][:, (dc % 2) * 256:(dc % 2) * 256 + W]
                S.op('pe', lambda e: e.matmul(po, lhsT=self.vb[k][:, dc * 128:(dc + 1) * 128], rhs=self.wT[k][:, 0:W],
                                              start=(i == 0), stop=(i == NCH - 1)),
                     reads=[self.b_vb[k], self.b_wT[k]], writes=[b_ps[dc // 2]])
        for dc in range(KC):
            po = ps[dc // 2][:, (dc % 2) * 256:(dc % 2) * 256 + W]
            S.op('dve', lambda e: e.tensor_tensor(out=hT[:, dc, c0:c0 + W], in0=hT[:, dc, c0:c0 + W], in1=po, op=ALU.add),
                 reads=[b_hT, b_ps[dc // 2]], writes=[b_hT])

    def run_phase(self, ar, layer, wq_l, sk_l, gvec, b_vec, tiles, hT, b_hT, uT_scr, v_scr, b_scr):
        kb = self.kb
        ar.phase()
        self.alloc(ar)
        self.load_layer(wq_l, sk_l)
        for (c0, W) in tiles:
            kb.rmsnorm_to(ar, hT, b_hT, c0, W, gvec, b_vec, self.xnT, self.b_xnT, tmp=(self.sq, self.b_sq, self.rstd, self.b_rstd))
            self.route_tile(W)
            self.expert_loop(W, uT_scr, v_scr, b_scr, hT, b_hT, c0)


V_MIX = 0
V_FFN = 2
V_PLE = 4
V_FIN = 6
V_BIN = 7
V_DWW = 9
V_DWB = 40
V_LNG = 41
V_LNB = 42
V_BOUT = 43
NVEC = 64


def conv_phase(kb, ar, R, b_R, vecT, b_vec, hT, b_hT, tiles, w_in_d, w_out_d, halo_src, flag_ap, b_tab,
               smp=None, tail_save=None, tail_out=None):
    S = kb.S
    ar.phase()
    kb.set_xtok(ar)
    a = ar.alloc
    w_in = R[:, 0:KC * 2048].rearrange('p (k n) -> p k n', k=KC)
    w_out = R[:, KC * 2048:KC * 3072].rearrange('p (k n) -> p k n', k=KC)
    kb.load_w(w_in, b_R, w_in_d)
    kb.load_w(w_out, b_R, w_out_d)
    xnT, b_xnT = a([128, KC, TT], BF16, 'xnT')
    sq, b_sq = a([128, KC, TT], F32, 'sq')
    rstd, b_rstd = a([128, TT], F32, 'rstd')
    gbuf, b_gbuf = a([128, KC, 32 + TT], F32, 'gbuf')
    acc, b_acc = a([128, KC, TT], F32, 'acc')
    sg, b_sg = [], []
    for i in range(2):
        x, b = a([128, TT], F32, 'sg'); sg.append(x); b_sg.append(b)
    ysT, b_ysT = a([128, KC, TT], BF16, 'ysT')
    mean, b_mean = a([128, TT], F32, 'mean')
    msq, b_msq = a([128, TT], F32, 'msq')
    lrs, b_lrs = a([128, TT], F32, 'lrs')
    t1, b_t1 = a([128, TT], F32, 't1')

    def vec(r):
        return vecT[:, :, r]

    if halo_src is None:
        S.op('pool', lambda e: e.memset(gbuf[:, :, 0:32], 0.0), writes=[b_gbuf])
    else:
        hs, b_hs = halo_src
        S.op('dve', lambda e: e.tensor_scalar(out=gbuf[:, :, 0:32], in0=hs, scalar1=flag_ap, scalar2=None, op0=ALU.mult),
             reads=[b_hs, b_tab], writes=[b_gbuf])

    def glu_tile(c0, W):
        kb.rmsnorm_to(ar, hT, b_hT, c0, W, vec(V_MIX + 0), b_vec, xnT, b_xnT, tmp=(sq, b_sq, rstd, b_rstd))
        for n in range(KC):
            ba, bb = kb.bank(), kb.bank()
            pa = kb.ps[ba][:, 0:W]
            pb = kb.ps[bb][:, 0:W]
            kb.lin(pa, kb.b_ps[ba], w_in, b_R, n * 128, 128, xnT, b_xnT, W)
            kb.lin(pb, kb.b_ps[bb], w_in, b_R, 1024 + n * 128, 128, xnT, b_xnT, W)
            k = n % 2
            S.op('act', lambda e: e.activation(out=sg[k][:, 0:W], in_=pb, func=AF.Sigmoid, bias=vecT[:, n, V_BIN + 1:V_BIN + 2]),
                 reads=[kb.b_ps[bb], b_vec], writes=[b_sg[k]])
            S.op('dve', lambda e: e.scalar_tensor_tensor(out=gbuf[:, n, 32:32 + W], in0=pa, scalar=vecT[:, n, V_BIN:V_BIN + 1], in1=sg[k][:, 0:W],
                                                         op0=ALU.add, op1=ALU.mult),
                 reads=[kb.b_ps[ba], b_sg[k], b_vec], writes=[b_gbuf])

    def ln_out_tile(c0, W):
        S.op('act', lambda e: e.activation(out=sq[:, :, 0:W], in_=acc[:, :, 0:W], func=AF.Square), reads=[b_acc], writes=[b_sq])
        p1 = kb.ps[7][:, 0:W]
        p2 = kb.ps[7][:, 256:256 + W]
        for kc in range(KC):
            S.op('pe', lambda e: e.matmul(p1, lhsT=kb.ones_f[:], rhs=acc[:, kc, 0:W], start=(kc == 0), stop=(kc == KC - 1)),
                 reads=[b_acc, kb.b_const], writes=[kb.b_ps[7]])
        for kc in range(KC):
            S.op('pe', lambda e: e.matmul(p2, lhsT=kb.ones_f[:], rhs=sq[:, kc, 0:W], start=(kc == 0), stop=(kc == KC - 1)),
                 reads=[b_sq, kb.b_const], writes=[kb.b_ps[7]])
        S.op('act', lambda e: e.activation(out=mean[:, 0:W], in_=p1, func=AF.Copy, scale=1.0 / D), reads=[kb.b_ps[7]], writes=[b_mean])
        S.op('dve', lambda e: e.tensor_tensor(out=msq[:, 0:W], in0=mean[:, 0:W], in1=mean[:, 0:W], op=ALU.mult), reads=[b_mean], writes=[b_msq])
        S.op('dve', lambda e: e.scalar_tensor_tensor(out=lrs[:, 0:W], in0=p2, scalar=1.0 / D, in1=msq[:, 0:W], op0=ALU.mult, op1=ALU.subtract),
             reads=[kb.b_ps[7], b_msq], writes=[b_lrs])
        S.op('act', lambda e: e.activation(out=lrs[:, 0:W], in_=lrs[:, 0:W], func=AF.Sqrt, bias=1e-6), reads=[b_lrs], writes=[b_lrs])
        S.op('dve', lambda e: e.reciprocal(out=lrs[:, 0:W], in_=lrs[:, 0:W]), reads=[b_lrs], writes=[b_lrs])
        for n in range(KC):
            S.op('dve', lambda e: e.tensor_tensor(out=t1[:, 0:W], in0=acc[:, n, 0:W], in1=mean[:, 0:W], op=ALU.subtract),
                 reads=[b_acc, b_mean], writes=[b_t1])
            S.op('dve', lambda e: e.tensor_tensor(out=t1[:, 0:W], in0=t1[:, 0:W], in1=lrs[:, 0:W], op=ALU.mult), reads=[b_t1, b_lrs], writes=[b_t1])
            S.op('act', lambda e: e.activation(out=ysT[:, n, 0:W], in_=t1[:, 0:W], func=AF.Silu, scale=vecT[:, n, V_LNG:V_LNG + 1],
                                               bias=vecT[:, n, V_LNB:V_LNB + 1]),
                 reads=[b_t1, b_vec], writes=[b_ysT])
        for n in range(KC):
            bi = kb.bank()
            po = kb.ps[bi][:, 0:W]
            kb.lin(po, kb.b_ps[bi], w_out, b_R, n * 128, 128, ysT, b_ysT, W)
            S.op('dve', lambda e: e.scalar_tensor_tensor(out=hT[:, n, c0:c0 + W], in0=po, scalar=vecT[:, n, V_BOUT:V_BOUT + 1], in1=hT[:, n, c0:c0 + W],
                                                         op0=ALU.add, op1=ALU.add),
                 reads=[kb.b_ps[bi], b_vec, b_hT], writes=[b_hT])

    for (c0, W) in tiles:
        glu_tile(c0, W)
        for n in range(KC):
            S.op('dve', lambda e: e.tensor_scalar(out=acc[:, n, 0:W], in0=gbuf[:, n, 2:2 + W], scalar1=vecT[:, n, V_DWW:V_DWW + 1],
                                                  scalar2=vecT[:, n, V_DWB:V_DWB + 1], op0=ALU.mult, op1=ALU.add),
                 reads=[b_gbuf, b_vec], writes=[b_acc])
            for w in range(1, 31):
                S.op('dve', lambda e: e.scalar_tensor_tensor(out=acc[:, n, 0:W], in0=gbuf[:, n, 2 + w:2 + w + W], scalar=vecT[:, n, V_DWW + w:V_DWW + w + 1],
                                                             in1=acc[:, n, 0:W], op0=ALU.mult, op1=ALU.add),
                     reads=[b_gbuf, b_vec, b_acc], writes=[b_acc])
        ln_out_tile(c0, W)
        S.op('pool', lambda e: e.tensor_copy(out=gbuf[:, :, 0:32], in_=gbuf[:, :, W:W + 32]), reads=[b_gbuf], writes=[b_gbuf])
    if tail_save is not None:
        S.op('pool', lambda e: e.tensor_copy(out=tail_save[0], in_=gbuf[:, :, 0:32]), reads=[b_gbuf], writes=[tail_save[1]])
    if tail_out is not None:
        kb.store_T(lambda kc: gbuf[:, kc, 2:32], b_gbuf, 30, tail_out[0], tail_out[1])

    if smp is not None:
        c0 = smp['c0']
        W = NS
        histT, b_histT = a([128, KC, NS * 30], F32, 'histT')
        sc = smp['state_conv']
        sc_rows = sc.rearrange('s w d -> (s w) d')
        for r0 in range(0, NS * 30, 120):
            kb.load_T(sc_rows[r0:r0 + 120, :], 120, lambda g0, nk: histT[:, g0:g0 + nk, r0:r0 + 120], b_histT)
        glu_tile(c0, W)
        gs = gbuf[:, :, 32:32 + W]
        tmp, b_tmp = histT, b_histT
        dwbase = vecT[:, 0, V_DWW:V_DWW + 1]
        S.op('dve', lambda e: e.tensor_tensor(out=tmp.rearrange('p k (s w) -> p k s w', s=NS), in0=histT.rearrange('p k (s w) -> p k s w', s=NS),
                                              in1=mkap(dwbase, 0, [[NVEC, KC], [0, NS], [1, 30]]), op=ALU.mult),
             reads=[b_histT, b_vec], writes=[b_tmp])
        S.op('dve', lambda e: e.tensor_reduce(out=acc[:, :, 0:W], in_=tmp.rearrange('p k (s w) -> p k s w', s=NS), axis=AX.X, op=ALU.add),
             reads=[b_tmp], writes=[b_acc])
        for n in range(KC):
            S.op('dve', lambda e: e.scalar_tensor_tensor(out=acc[:, n, 0:W], in0=gbuf[:, n, 32:32 + W], scalar=vecT[:, n, V_DWW + 30:V_DWW + 31],
                                                         in1=acc[:, n, 0:W], op0=ALU.mult, op1=ALU.add),
                 reads=[b_gbuf, b_vec, b_acc], writes=[b_acc])
            S.op('dve', lambda e: e.tensor_scalar(out=acc[:, n, 0:W], in0=acc[:, n, 0:W], scalar1=vecT[:, n, V_DWB:V_DWB + 1], scalar2=None, op0=ALU.add),
                 reads=[b_acc, b_vec], writes=[b_acc])
        ln_out_tile(c0, W)
        outs = smp['out_state']
        b_o = smp['b_out']
        S.dma('sp', lambda e: e.dma_start(out=outs[:, 0:29, :], in_=sc[:, 1:30, :]), writes=[b_o], final=True)
        kb.store_T(lambda kc: gbuf[:, kc, 32:32 + W], b_gbuf, W, outs[:, 29, :], b_o)
    return


def ple_phase(kb, ar, R, b_R, vecT, b_vec, layer, hT, b_hT, tiles, wg_d, wp_d, p_src):
    S = kb.S
    ar.phase()
    kb.set_xtok(ar)
    a = ar.alloc
    wg = R[:, 0:KC * 1024].rearrange('p (k n) -> p k n', k=KC)
    wp = R[:, KC * 1024:KC * 1024 + 2 * 1024].rearrange('p (k n) -> p k n', k=2)
    kb.load_w(wg, b_R, wg_d)
    kb.load_w(wp, b_R, wp_d, nkc=2)
    xnT, b_xnT = a([128, KC, TT], BF16, 'xnT')
    sq, b_sq = a([128, KC, TT], F32, 'sq')
    rstd, b_rstd = a([128, TT], F32, 'rstd')
    pT, b_pT = a([128, 2, TT], BF16, 'pT')
    sg, b_sg, tm, b_tm = [], [], [], []
    for i in range(2):
        x, b = a([128, TT], F32, 'sg'); sg.append(x); b_sg.append(b)
        x, b = a([128, TT], F32, 'tm'); tm.append(x); b_tm.append(b)
    for (c0, W) in tiles:
        kb.rmsnorm_to(ar, hT, b_hT, c0, W, vecT[:, :, V_PLE + layer], b_vec, xnT, b_xnT, tmp=(sq, b_sq, rstd, b_rstd))
        for (rows, P, off) in p_src(c0, W):
            kb.load_T(rows, P, lambda g0, nk: pT[:, g0:g0 + nk, off:off + P], b_pT, ncols=256)
        for n in range(KC):
            ba, bb = kb.bank(), kb.bank()
            pa = kb.ps[ba][:, 0:W]
            pb = kb.ps[bb][:, 0:W]
            kb.lin(pa, kb.b_ps[ba], wg, b_R, n * 128, 128, xnT, b_xnT, W)
            kb.lin(pb, kb.b_ps[bb], wp, b_R, n * 128, 128, pT, b_pT, W, nkc=2)
            k = n % 2
            S.op('act', lambda e: e.activation(out=sg[k][:, 0:W], in_=pa, func=AF.Sigmoid), reads=[kb.b_ps[ba]], writes=[b_sg[k]])
            S.op('dve', lambda e: e.tensor_tensor(out=tm[k][:, 0:W], in0=sg[k][:, 0:W], in1=pb, op=ALU.mult),
                 reads=[b_sg[k], kb.b_ps[bb]], writes=[b_tm[k]])
            S.op('pool', lambda e: e.tensor_tensor(out=hT[:, n, c0:c0 + W], in0=hT[:, n, c0:c0 + W], in1=tm[k][:, 0:W], op=ALU.add),
                 reads=[b_hT, b_tm[k]], writes=[b_hT])


def qkv_phase(kb, ar, R, b_R, vecT, b_vec, hT, b_hT, tiles, wqkv_d, KT_scr, V_scr, b_kv, QT_scr=None, k_out=None, v_out=None, b_out=None,
              smp=None):
    S = kb.S
    ar.phase()
    kb.set_xtok(ar)
    a = ar.alloc
    wqkv = R[:, 0:KC * 3072].rearrange('p (k n) -> p k n', k=KC)
    kb.load_w(wqkv[:, :, 0:1536], b_R, wqkv_d[:, 0:1536])
    kb.load_w(wqkv[:, :, 1536:3072], b_R, wqkv_d[:, 1536:3072])
    xnT, b_xnT = a([128, KC, TT], BF16, 'xnT')
    sq, b_sq = a([128, KC, TT], F32, 'sq')
    rstd, b_rstd = a([128, TT], F32, 'rstd')
    ft, b_ft = [], []
    for i in range(2):
        x, b = a([128, TT], BF16, 'ft'); ft.append(x); b_ft.append(b)
    tokf, b_tokf = [], []
    for i in range(2):
        x, b = a([128, 1024], F32, 'tokf'); tokf.append(x); b_tokf.append(b)
    tokb, b_tokb = [], []
    for i in range(2):
        x, b = a([128, 1024], BF16, 'tokb'); tokb.append(x); b_tokb.append(b)
    nft = 0
    ntk = 0
    for (c0, W, row0) in tiles:
        kb.rmsnorm_to(ar, hT, b_hT, c0, W, vecT[:, :, V_MIX + 1], b_vec, xnT, b_xnT, tmp=(sq, b_sq, rstd, b_rstd))
        is_smp = row0 is None
        if not is_smp:
            groups = ([(0, QT_scr, 0.125)] if QT_scr is not None else []) + [(1024, KT_scr, 1.0)]
            for (coff, scr, scl) in groups:
                for hp in range(8):
                    bi = kb.bank()
                    po = kb.ps[bi][:, 0:W]
                    kb.lin(po, kb.b_ps[bi], wqkv, b_R, coff + hp * 128, 128, xnT, b_xnT, W)
                    k = nft % 2
                    nft += 1
                    S.op('act', lambda e: e.activation(out=ft[k][:, 0:W], in_=po, func=AF.Copy, scale=scl), reads=[kb.b_ps[bi]], writes=[b_ft[k]])
                    S.dma('sp', lambda e: e.dma_start(out=scr[hp, :, row0:row0 + W], in_=ft[k][:, 0:W]), reads=[b_ft[k]], writes=[b_kv])
        nsub = (W + 127) // 128
        for sub in range(nsub):
            P = min(128, W - sub * 128)
            t0 = sub * 128
            which = [(1024, 'k'), (2048, 'v')] + ([(0, 'q')] if is_smp else [])
            for (coff, nm) in which:
                k = ntk % 2
                ntk += 1
                if is_smp:
                    dstf = smp[nm + 'tok']
                    b_dstf = smp['b_' + nm]
                else:
                    dstf = tokf[k]
                    b_dstf = b_tokf[k]
                for half in range(2):
                    bi = kb.bank()
                    po = kb.ps[bi][0:P, :]
                    for kc in range(KC):
                        S.op('pe', lambda e: e.matmul(po, lhsT=xnT[:, kc, t0:t0 + P], rhs=wqkv[:, kc, coff + half * 512:coff + (half + 1) * 512],
                                                      start=(kc == 0), stop=(kc == KC - 1)),
                             reads=[b_xnT, b_R], writes=[kb.b_ps[bi]])
                    scl = 0.125 if nm == 'q' else 1.0
                    if k == 0:
                        S.op('act', lambda e: e.activation(out=dstf[0:P, half * 512:(half + 1) * 512], in_=po, func=AF.Copy, scale=scl),
                             reads=[kb.b_ps[bi]], writes=[b_dstf])
                    else:
                        S.op('dve', lambda e: e.tensor_scalar(out=dstf[0:P, half * 512:(half + 1) * 512], in0=po, scalar1=scl, scalar2=None, op0=ALU.mult),
                             reads=[kb.b_ps[bi]], writes=[b_dstf])
                if is_smp:
                    if nm in ('k', 'v'):
                        S.dma('sp', lambda e: e.dma_start(out=smp[nm + '_out'], in_=dstf[0:P, :]), reads=[b_dstf], writes=[smp['b_out']], final=True)
                else:
                    r = row0 + t0
                    if k_out is not None:
                        S.dma('sp', lambda e: e.dma_start(out=(k_out if nm == 'k' else v_out)[r:r + P, :], in_=dstf[0:P, :]),
                              reads=[b_dstf], writes=[b_out], final=True)
                    if nm == 'v':
                        S.op('pool', lambda e: e.tensor_copy(out=tokb[k][0:P, :], in_=dstf[0:P, :]), reads=[b_dstf], writes=[b_tokb[k]])
                        S.dma('sp', lambda e: e.dma_start(out=V_scr[r:r + P, :], in_=tokb[k][0:P, :]), reads=[b_tokb[k]], writes=[b_kv])


def attn_prompt(kb, ar, OT, b_OT, KT_ctx, KT_own, V_ctx, V_own, QT_scr, b_kv, band_d, c31, b_tab, vmask, notown):
    S = kb.S
    ar.phase()
    a = ar.alloc
    KTh, b_KTh = a([128, 4096], BF16, 'KTh')
    QTh, b_QTh = a([128, 2048], BF16, 'QTh')
    Vh, b_Vh = a([128, 32, 65], BF16, 'Vh')
    Bh, b_Bh = a([128, 1024], BF16, 'Bh')
    ksum, b_ksum = a([128, 16], F32, 'ksum')
    ksumb, b_ksumb = a([128, 16], BF16, 'ksumb')
    gm, b_gm = a([128, 16], F32, 'gm')
    top8, b_top8 = a([128, 8], F32, 'top8')
    thr, b_thr = a([128, 1], F32, 'thr')
    negm, b_negm = a([128, 16], F32, 'negm')
    negT, b_negT = a([128, 512], BF16, 'negT')
    Ind, b_Ind = a([128, 16, 128], BF16, 'Ind')
    PT, b_PT = [], []
    for i in range(2):
        x, b = a([128, 512], BF16, 'PT'); PT.append(x); b_PT.append(b)
    Otok, b_Otok = a([128, 16, 128], BF16, 'Otok')
    rden, b_rden = a([128, 4], F32, 'rden')
    S.op('dve', lambda e: e.tensor_copy(out=Ind[0:16], in_=mkap(kb.ident_f[0:16, 0:1], 0, [[1, 16], [0, 128]])),
         reads=[kb.b_const], writes=[b_Ind])
    S.op('pool', lambda e: e.memset(Vh[:, :, 64:65], 1.0), writes=[b_Vh])
    nPT = 0
    for h in range(16):
        hp, r0 = h // 2, (h % 2) * 64
        S.dma('sp', lambda e: e.dma_start(out=KTh[0:64, 0:2048], in_=KT_ctx[hp, r0:r0 + 64, :]), reads=[b_kv], writes=[b_KTh])
        S.dma('sp', lambda e: e.dma_start(out=KTh[0:64, 2048:4096], in_=KT_own[hp, r0:r0 + 64, 0:2048]), reads=[b_kv], writes=[b_KTh])
        S.dma('sp', lambda e: e.dma_start(out=QTh[0:64, :], in_=QT_scr[hp, r0:r0 + 64, 0:2048]), reads=[b_kv], writes=[b_QTh])
        S.dma('sp', lambda e: e.dma_start(out=Vh[:, 0:16, 0:64], in_=V_ctx.rearrange('(kt p) c -> p kt c', p=128)[:, :, h * 64:(h + 1) * 64]),
              reads=[b_kv], writes=[b_Vh])
        S.dma('sp', lambda e: e.dma_start(out=Vh[:, 16:32, 0:64], in_=V_own.rearrange('(kt p) c -> p kt c', p=128)[:, :, h * 64:(h + 1) * 64]),
              reads=[b_kv], writes=[b_Vh])
        S.dma('pool', lambda e: e.dma_start(out=Bh, in_=band_d[h]), writes=[b_Bh])
        S.op('dve', lambda e: e.tensor_reduce(out=ksum[0:64, :], in_=KTh[0:64, :].rearrange('p (b k) -> p b k', b=16), axis=AX.X, op=ALU.add),
             reads=[b_KTh], writes=[b_ksum])
        S.op('dve', lambda e: e.tensor_copy(out=ksumb[0:64, :], in_=ksum[0:64, :]), reads=[b_ksum], writes=[b_ksumb])
        for g in range(4):
            q0 = 512 * g
            for j in range(4):
                qt = 4 * g + j
                pg = kb.ps[7]
                S.op('pe', lambda e: e.matmul(pg[:, 0:16], lhsT=QTh[0:64, qt * 128:(qt + 1) * 128], rhs=ksumb[0:64, :], start=True, stop=True),
                     reads=[b_QTh, b_ksumb], writes=[kb.b_ps[7]])
                S.op('dve', lambda e: e.tensor_tensor(out=gm, in0=pg[:, 0:16], in1=vmask[:, qt, :], op=ALU.add), reads=[kb.b_ps[7], b_tab], writes=[b_gm])
                S.op('dve', lambda e: e.max(out=top8, in_=gm), reads=[b_gm], writes=[b_top8])
                S.op('dve', lambda e: e.tensor_single_scalar(out=thr, in_=top8[:, 2:3], scalar=-1e30, op=ALU.max), reads=[b_top8], writes=[b_thr])
                S.op('dve', lambda e: e.tensor_scalar(out=negm, in0=gm, scalar1=thr[:, 0:1], scalar2=-30000.0, op0=ALU.is_lt, op1=ALU.mult),
                     reads=[b_gm, b_thr], writes=[b_negm])
                S.op('dve', lambda e: e.tensor_tensor(out=negm, in0=negm, in1=notown[:, qt, :], op=ALU.mult), reads=[b_negm, b_tab], writes=[b_negm])
                S.op('pe', lambda e: e.transpose(out=pg[0:16, 128:256], in_=negm, identity=kb.ident_f[:]), reads=[b_negm, kb.b_const], writes=[kb.b_ps[7]])
                S.op('act', lambda e: e.activation(out=negT[0:16, j * 128:(j + 1) * 128], in_=pg[0:16, 128:256], func=AF.Copy),
                     reads=[kb.b_ps[7]], writes=[b_negT])
            nkt = 16 + 4 * g + 4
            pob = 5 + (g % 2)
            pO = kb.ps[pob]
            last_kt = [min(nkt - 1, 16 + 4 * g + j) for j in range(4)]
            for kt in range(nkt):
                near = kt >= 15 + 4 * g
                bi = kb.bank()
                pS = kb.ps[bi]
                S.op('pe', lambda e: e.matmul(pS[:, :], lhsT=KTh[0:64, kt * 128:(kt + 1) * 128], rhs=QTh[0:64, q0:q0 + 512], start=True, stop=False),
                     reads=[b_KTh, b_QTh], writes=[kb.b_ps[bi]])
                S.op('pe', lambda e: e.matmul(pS[:, :], lhsT=Ind[0:16, kt // 2, :], rhs=negT[0:16, :], start=False, stop=(not near)),
                     reads=[b_Ind, b_negT], writes=[kb.b_ps[bi]])
                if near:
                    delta = (2048 + q0) - 128 * kt
                    off = delta + 384
                    S.op('pe', lambda e: e.matmul(pS[:, :], lhsT=kb.ident_b[:], rhs=Bh[:, off:off + 512], start=False, stop=True),
                         reads=[b_Bh, kb.b_const], writes=[kb.b_ps[bi]])
                k = nPT % 2
                nPT += 1
                if near:
                    S.op('act', lambda e: e.activation(out=PT[k], in_=pS[:, :], func=AF.Exp), reads=[kb.b_ps[bi]], writes=[b_PT[k]])
                else:
                    S.op('act', lambda e: e.activation(out=PT[k], in_=pS[:, :], func=AF.Exp, bias=c31[:, h:h + 1]),
                         reads=[kb.b_ps[bi], b_tab], writes=[b_PT[k]])
                for j in range(4):
                    if kt > last_kt[j]:
                        continue
                    S.op('pe', lambda e: e.matmul(pO[:, j * 65:(j + 1) * 65], lhsT=PT[k][:, j * 128:(j + 1) * 128], rhs=Vh[:, kt, :],
                                                  start=(kt == 0), stop=(kt == last_kt[j])),
                         reads=[b_PT[k], b_Vh], writes=[kb.b_ps[pob]])
            pO3 = pO[:, 0:260].rearrange('p (j c) -> p j c', j=4)
            S.op('dve', lambda e: e.reciprocal(out=rden, in_=pO3[:, :, 64]), reads=[kb.b_ps[pob]], writes=[b_rden])
            S.op('dve', lambda e: e.tensor_tensor(out=Otok[:, 4 * g:4 * g + 4, r0:r0 + 64], in0=pO3[:, :, 0:64], in1=mkap(rden[:, 0:1], 0, [[1, 4], [0, 64]]),
                                                  op=ALU.mult),
                 reads=[kb.b_ps[pob], b_rden], writes=[b_Otok])
        if h % 2 == 1:
            for qt in range(16):
                pt = kb.ps[7]
                ptv = pt[:].bitcast(BF16)
                S.op('pe', lambda e: e.transpose(out=ptv[:, 0:128], in_=Otok[:, qt, :], identity=kb.ident_b[:]),
                     reads=[b_Otok, kb.b_const], writes=[kb.b_ps[7]])
                S.op('act', lambda e: e.activation(out=OT[:, hp, qt * 128:(qt + 1) * 128], in_=ptv[:, 0:128], func=AF.Copy),
                     reads=[kb.b_ps[7]], writes=[b_OT])


def attn_sample(kb, ar, OT, b_OT, smp, ck_rows, cv_rows, pt_d, sbias, b_tab, o_scr, b_oscr):
    S = kb.S
    ar.phase()
    kb.set_xtok(ar)
    a = ar.alloc
    ptb, b_ptb = a([128, 256], I32, 'ptb')
    ptf, b_ptf = a([128, 256], F32, 'ptf')
    idx, b_idx = a([128, 256], I32, 'idx')
    Kpg, b_Kpg, Vpg, b_Vpg, Vb, b_Vb = [], [], [], [], [], []
    for i in range(2):
        x, b = a([128, 1024], F32, 'Kpg'); Kpg.append(x); b_Kpg.append(b)
        x, b = a([128, 1024], F32, 'Vpg'); Vpg.append(x); b_Vpg.append(b)
        x, b = a([128, 1024], BF16, 'Vb'); Vb.append(x); b_Vb.append(b)
    prod, b_prod = a([128, 1024], F32, 'prod')
    qb, b_qb = a([128, 1024], F32, 'qb')
    lg, b_lg = a([128, 17, 16], F32, 'lg')
    Pb, b_Pb = a([128, 17, 16], BF16, 'Pb')
    gsum, b_gsum = a([128, 128], F32, 'gsum')
    cmp, b_cmp = prod, b_prod
    cnt, b_cnt = a([128, 128], F32, 'cnt')
    negrow, b_negrow = a([128, 128], F32, 'negrow')
    SelA, b_SelA = a([128, 16, 128], F32, 'SelA')
    E0, b_E0 = a([128, 16, 128], F32, 'E0')
    otmp, b_otmp = qb, b_qb
    osb, b_osb = a([128, 64], F32, 'osb')
    rd, b_rd = a([128, 1], F32, 'rd')
    S.dma('sp', lambda e: e.dma_start(out=ptb, in_=pt_d.rearrange('s g -> (s g)').partition_broadcast(128)), writes=[b_ptb])
    S.op('dve', lambda e: e.tensor_copy(out=ptf, in_=ptb), reads=[b_ptb], writes=[b_ptf])
    S.op('dve', lambda e: e.tensor_scalar(out=ptf, in0=ptf, scalar1=128.0, scalar2=kb.iota_p[:, 0:1], op0=ALU.mult, op1=ALU.add),
         reads=[b_ptf, kb.b_const], writes=[b_ptf])
    S.op('dve', lambda e: e.tensor_copy(out=idx, in_=ptf), reads=[b_ptf], writes=[b_idx])
    S.op('dve', lambda e: e.tensor_copy(out=SelA[0:16], in_=mkap(kb.ident_f[0:16, 0:1], 0, [[1, 16], [0, 128]])), reads=[kb.b_const], writes=[b_SelA])
    S.op('pool', lambda e: e.memset(E0[0:16], 0.0), writes=[b_E0])
    S.op('dve', lambda e: e.tensor_copy(out=E0[0:16, :, 0], in_=kb.ident_f[0:16, 0:16]), reads=[kb.b_const], writes=[b_E0])
    qtok, ktok, vtok = smp['qtok'], smp['ktok'], smp['vtok']
    b_q, b_k, b_v = smp['b_q'], smp['b_k'], smp['b_v']
    import os
    SS = int(os.environ.get('SSTAGE', '99'))
    npg = 0
    if SS < 1:
        return
    for s in range(int(os.environ.get('NSMP', NS))):
        for half in range(2):
            bi = kb.bank()
            po = kb.ps[bi]
            S.op('pe', lambda e: e.matmul(po[:, :], lhsT=SelA[0:16, s, :], rhs=qtok[0:16, half * 512:(half + 1) * 512], start=True, stop=True),
                 reads=[b_SelA, b_q], writes=[kb.b_ps[bi]])
            S.op('act', lambda e: e.activation(out=qb[:, half * 512:(half + 1) * 512], in_=po[:, :], func=AF.Copy), reads=[kb.b_ps[bi]], writes=[b_qb])
        for pg in range(17):
            k = npg % 2
            npg += 1
            if pg < 16:
                col = s * 16 + pg
                S.dma('pool', lambda e: e.indirect_dma_start(out=Kpg[k], out_offset=None, in_=ck_rows,
                                                             in_offset=bass.IndirectOffsetOnAxis(ap=idx[:, col:col + 1], axis=0)),
                      reads=[b_idx], writes=[b_Kpg[k]])
            else:
                for half in range(2):
                    bi = kb.bank()
                    po = kb.ps[bi]
                    S.op('pe', lambda e: e.matmul(po[:, :], lhsT=E0[0:16, s, :], rhs=ktok[0:16, half * 512:(half + 1) * 512], start=True, stop=True),
                         reads=[b_E0, b_k], writes=[kb.b_ps[bi]])
                    S.op('act', lambda e: e.activation(out=Kpg[k][:, half * 512:(half + 1) * 512], in_=po[:, :], func=AF.Copy),
                         reads=[kb.b_ps[bi]], writes=[b_Kpg[k]])
            S.op('pool', lambda e: e.tensor_tensor(out=prod, in0=Kpg[k], in1=qb, op=ALU.mult), reads=[b_Kpg[k], b_qb], writes=[b_prod])
            S.op('dve', lambda e: e.tensor_reduce(out=lg[:, pg, :], in_=prod.rearrange('p (h d) -> p h d', h=16), axis=AX.X, op=ALU.add),
                 reads=[b_prod], writes=[b_lg])
        if SS < 2:
            continue
        pg_ = kb.ps[7]
        S.op('pe', lambda e: e.matmul(pg_[0:1, 0:256], lhsT=kb.ones_f[:, 0:1], rhs=lg[:, 0:16, :].rearrange('p g h -> p (g h)'), start=True, stop=True),
             reads=[b_lg, kb.b_const], writes=[kb.b_ps[7]])
        ps4 = pg_[0:1, 0:256].rearrange('p (b t h) -> p b t h', b=8, t=2)
        S.op('act', lambda e: e.activation(out=gsum[0:1, :].rearrange('p (b h) -> p b h', b=8), in_=ps4[:, :, 0, :], func=AF.Copy),
             reads=[kb.b_ps[7]], writes=[b_gsum])
        S.op('dve', lambda e: e.tensor_tensor(out=gsum[0:1, :].rearrange('p (b h) -> p b h', b=8), in0=gsum[0:1, :].rearrange('p (b h) -> p b h', b=8),
                                              in1=ps4[:, :, 1, :], op=ALU.add),
             reads=[kb.b_ps[7], b_gsum], writes=[b_gsum])
        g0 = gsum[0:1, 0:1]
        S.op('dve', lambda e: e.tensor_tensor(out=cmp[0:1, :].rearrange('p (h b c) -> p h b c', h=16, b=8),
                                              in0=mkap(g0, 0, [[1, 16], [0, 8], [16, 8]]), in1=mkap(g0, 0, [[1, 16], [16, 8], [0, 8]]), op=ALU.is_gt),
             reads=[b_gsum], writes=[b_cmp])
        S.op('dve', lambda e: e.tensor_reduce(out=cnt[0:1, :], in_=cmp[0:1, :].rearrange('p (m c) -> p m c', c=8), axis=AX.X, op=ALU.add),
             reads=[b_cmp], writes=[b_cnt])
        S.op('dve', lambda e: e.tensor_scalar(out=negrow[0:1, :], in0=cnt[0:1, :], scalar1=3.0, scalar2=-30000.0, op0=ALU.is_ge, op1=ALU.mult),
             reads=[b_cnt], writes=[b_negrow])
        S.op('pe', lambda e: e.matmul(pg_[:, 256:384], lhsT=kb.ones_f[0:1, :], rhs=negrow[0:1, :], start=True, stop=True),
             reads=[b_negrow, kb.b_const], writes=[kb.b_ps[7]])
        lg4 = lg[:, 0:16, :].rearrange('p (b t) h -> p b t h', t=2)
        S.op('dve', lambda e: e.tensor_tensor(out=lg4, in0=lg4, in1=mkap(pg_[:, 256:257], 0, [[1, 8], [0, 2], [8, 16]]), op=ALU.add),
             reads=[b_lg, kb.b_ps[7]], writes=[b_lg])
        S.op('dve', lambda e: e.tensor_tensor(out=lg, in0=lg, in1=sbias, op=ALU.add), reads=[b_lg, b_tab], writes=[b_lg])
        S.op('act', lambda e: e.activation(out=Pb, in_=lg, func=AF.Exp), reads=[b_lg], writes=[b_Pb])
        if SS < 3:
            continue
        pO = [kb.ps[5], kb.ps[6]]
        pD = kb.ps[7]
        for pg in range(17):
            k = npg % 2
            npg += 1
            if pg < 16:
                col = s * 16 + pg
                S.dma('pool', lambda e: e.indirect_dma_start(out=Vpg[k], out_offset=None, in_=cv_rows,
                                                             in_offset=bass.IndirectOffsetOnAxis(ap=idx[:, col:col + 1], axis=0)),
                      reads=[b_idx], writes=[b_Vpg[k]])
                S.op('pool', lambda e: e.tensor_copy(out=Vb[k], in_=Vpg[k]), reads=[b_Vpg[k]], writes=[b_Vb[k]])
            else:
                for half in range(2):
                    bi = kb.bank()
                    po = kb.ps[bi]
                    S.op('pe', lambda e: e.matmul(po[:, :], lhsT=E0[0:16, s, :], rhs=vtok[0:16, half * 512:(half + 1) * 512], start=True, stop=True),
                         reads=[b_E0, b_v], writes=[kb.b_ps[bi]])
                    S.op('act', lambda e: e.activation(out=Vb[k][:, half * 512:(half + 1) * 512], in_=po[:, :], func=AF.Copy),
                         reads=[kb.b_ps[bi]], writes=[b_Vb[k]])
            for half in range(2):
                S.op('pe', lambda e: e.matmul(pO[half][0:16, :], lhsT=Pb[:, pg, :], rhs=Vb[k][:, half * 512:(half + 1) * 512], start=(pg == 0), stop=(pg == 16)),
                     reads=[b_Pb, b_Vb[k]], writes=[kb.b_ps[5 + half]])
            S.op('pe', lambda e: e.matmul(pD[0:16, 400:401], lhsT=Pb[:, pg, :], rhs=kb.ones_b[:, 0:1], start=(pg == 0), stop=(pg == 16)),
                 reads=[b_Pb, kb.b_const], writes=[kb.b_ps[7]])
        if SS < 4:
            continue
        for half in range(2):
            S.op('dve', lambda e: e.tensor_tensor(out=otmp[0:16, half * 512:(half + 1) * 512].rearrange('p (h d) -> p h d', h=8),
                                                  in0=pO[half][0:16, :].rearrange('p (h d) -> p h d', h=8),
                                                  in1=mkap(kb.ident_f[0:16, half * 8:half * 8 + 1], 0, [[1, 8], [0, 64]]), op=ALU.mult),
                 reads=[kb.b_ps[5 + half], kb.b_const], writes=[b_otmp])
        S.op('dve', lambda e: e.tensor_reduce(out=osb[0:16, :], in_=mkap(otmp[0:16, 0:1], 0, [[1, 64], [64, 16]]), axis=AX.X, op=ALU.add),
             reads=[b_otmp], writes=[b_osb])
        S.op('dve', lambda e: e.reciprocal(out=rd[0:16, :], in_=pD[0:16, 400:401]), reads=[kb.b_ps[7]], writes=[b_rd])
        S.op('dve', lambda e: e.tensor_scalar(out=osb[0:16, :], in0=osb[0:16, :], scalar1=rd[0:16, 0:1], scalar2=None, op0=ALU.mult),
             reads=[b_osb, b_rd], writes=[b_osb])
        S.dma('sp', lambda e: e.dma_start(out=o_scr[s].rearrange('(h d) -> h d', h=16), in_=osb[0:16, :]), reads=[b_osb], writes=[b_oscr])
    tmpT, b_tmpT = a([128, KC, NS], F32, 'tmpT')
    kb.load_T(o_scr, NS, lambda g0, nk: tmpT[:, g0:g0 + nk, :], b_tmpT, b_src=b_oscr)
    kb.S.op('dve', lambda e: e.tensor_copy(out=OT[:, :, NP_OWN:NP_OWN + NS], in_=tmpT), reads=[b_tmpT], writes=[b_OT])
    return


def wo_phase(kb, ar, R, b_R, wo_view, OT, b_OT, hT, b_hT, tiles, wo_d):
    S = kb.S
    kb.load_w(wo_view, b_R, wo_d)
    for (c0, W) in tiles:
        for n in range(KC):
            bi = kb.bank()
            po = kb.ps[bi][:, 0:W]
            for kc in range(KC):
                S.op('pe', lambda e: e.matmul(po, lhsT=wo_view[:, kc, n * 128:(n + 1) * 128], rhs=OT[:, kc, c0:c0 + W], start=(kc == 0), stop=(kc == KC - 1)),
                     reads=[b_R, b_OT], writes=[kb.b_ps[bi]])
            S.op('dve', lambda e: e.tensor_tensor(out=hT[:, n, c0:c0 + W], in0=hT[:, n, c0:c0 + W], in1=po, op=ALU.add),
                 reads=[b_hT, kb.b_ps[bi]], writes=[b_hT])


def final_phase(kb, ar, vecT, b_vec, hT, b_hT, tiles_rows, b_out):
    S = kb.S
    ar.phase()
    kb.set_xtok(ar)
    a = ar.alloc
    sq, b_sq = a([128, KC, TT], F32, 'sq')
    rstd, b_rstd = a([128, TT], F32, 'rstd')
    yT, b_yT = a([128, KC, TT], F32, 'yT')
    for (c0, W, rows) in tiles_rows:
        kb.rmsnorm_to(ar, hT, b_hT, c0, W, vecT[:, :, V_FIN], b_vec, yT, b_yT, tmp=(sq, b_sq, rstd, b_rstd))
        for s0 in range(0, W, 128):
            P = min(128, W - s0)
            kb.store_T(lambda kc: yT[:, kc, s0:s0 + P], b_yT, P, rows[s0:s0 + P, :], b_out)


def build_program(stop_after=99, dbg=False, cache_rows=2560 * 128):
    nc = bass.Bass("TRN2", target_bir_lowering=False)

    def din(name, shape, dt=F32):
        return nc.dram_tensor(name, list(shape), dt, kind="ExternalInput").ap()

    def dout(name, shape, dt=F32):
        return nc.dram_tensor(name, list(shape), dt, kind="ExternalOutput").ap()

    def dint(name, shape, dt):
        return nc.dram_tensor(name, list(shape), dt, kind="Internal").ap()

    x_ctx = din('x_ctx', [2048, 1024]); x_own = din('x_own', [2048, 1024]); x_smp = din('x_smp', [NS, 1024])
    p_ctx0 = din('p_ctx0', [2048, 256]); p_own = din('p_own', [2, 2048, 256]); p_smp = din('p_smp', [2, NS, 256])
    state_conv = din('state_conv', [NS, 30, 1024])
    cache_k = din('cache_k', [cache_rows, 1024]); cache_v = din('cache_v', [cache_rows, 1024])
    page_table = din('page_table', [NS, 16], I32)
    vecs = din('vecs', [NVEC, 1024])
    w_in = din('w_in', [1024, 2048]); w_out = din('w_out', [1024, 1024])
    wqkv = din('wqkv', [1024, 3072]); wo = din('wo', [1024, 1024])
    peer_wq = din('peer_wq', [2, 1024, 1024]); sub_keys = din('sub_keys', [2, 8, 2, 128, 64])
    peer_u = din('peer_u', [2, 16384, 1024]); peer_v = din('peer_v', [2, 16384, 1024])
    ple_wp = din('ple_wp', [2, 256, 1024]); ple_wg = din('ple_wg', [2, 1024, 1024])
    flags_d = din('flags', [128, 1]); vmask_d = din('vmask', [128, 16, 16]); notown_d = din('notown', [128, 16, 16])
    c31_d = din('c31', [128, 16]); sbias_d = din('sbias', [128, 17, 16]); band_d = din('band', [16, 128, 1024])

    y_own = dout('y_own', [2048, 1024]); y_smp = dout('y_smp', [NS, 1024])
    conv_tail = dout('conv_tail', [30, 1024]); conv_smp = dout('conv_smp', [NS, 30, 1024])
    k_own = dout('k_own', [2048, 1024]); v_own = dout('v_own', [2048, 1024])
    k_smp = dout('k_smp', [NS, 1024]); v_smp = dout('v_smp', [NS, 1024])
    if dbg:
        dbg_h = dout('dbg_h', [NCOL, 1024])

    uT_scr = [dint('uT_scr%d' % l, [128, 128, 1024], BF16) for l in range(2)]
    v_scr = [dint('v_scr%d' % l, [16384, 1024], BF16) for l in range(2)]
    KT_ctx = dint('KT_ctx', [8, 128, 2048], BF16); KT_own = dint('KT_own', [8, 128, 2048], BF16)
    V_ctx = dint('V_ctx', [2048, 1024], BF16); V_own = dint('V_own', [2048, 1024], BF16)
    QT_scr = dint('QT_scr', [8, 128, 2048], BF16)
    o_scr = dint('o_scr', [NS, 1024], F32)

    with contextlib.ExitStack() as st:
        kb = KB(nc, st)
        S = kb.S
        hT = kb.sb('hT', [128, KC, NCOL], F32); b_hT = Buf('hT')
        R = kb.sb('R', [128, 128 * TT], BF16); b_R = Buf('R')
        vecT = kb.sb('vecT', [128, KC, NVEC], F32); b_vec = Buf('vecT')
        gtail = kb.sb('gtail', [128, KC, 32], F32); b_gtail = Buf('gtail')
        flags = kb.sb('flags_sb', [128, 1], F32)
        vmask = kb.sb('vmask_sb', [128, 16, 16], F32); notown = kb.sb('notown_sb', [128, 16, 16], F32)
        c31 = kb.sb('c31_sb', [128, 16], F32); sbias = kb.sb('sbias_sb', [128, 17, 16], F32)
        b_tab = Buf('tab')
        for (dst, src) in ((flags, flags_d), (vmask, vmask_d), (notown, notown_d), (c31, c31_d), (sbias, sbias_d)):
            S.dma('sp', lambda e: e.dma_start(out=dst[:], in_=src), writes=[b_tab])
        ar = Arena(kb, ARENA_F32)
        ar.phase()
        kb.set_xtok(ar)
        kb.load_T(vecs, NVEC, lambda g0, nk: vecT[:, g0:g0 + nk, :], b_vec)
        pe = Peer(kb, R)
        pe.b_Gt = b_R
        b_scr = [Buf('scr0'), Buf('scr1')]
        b_kv = Buf('kv'); b_out = Buf('out'); b_oscr = Buf('oscr')
        ptiles = [(c0, TT) for c0 in range(0, NP_OWN, TT)]
        stiles = [(NP_OWN, NS)]

        def dump_and_finish():
            if dbg:
                ar.phase(); kb.set_xtok(ar)
                for s0 in range(0, NCOL, 128):
                    P = min(128, NCOL - s0)
                    kb.store_T(lambda kc: hT[:, kc, s0:s0 + P], b_hT, P, dbg_h[s0:s0 + P, :], b_out)
            print("op counts", S.cnt, S.dcnt)
            S.emit()
            return nc

        def load_x(src, c0, n):
            for s0 in range(0, n, 128):
                P = min(128, n - s0)
                kb.load_T(src[s0:s0 + P, :], P, lambda g0, nk: hT[:, g0:g0 + nk, c0 + s0:c0 + s0 + P], b_hT)

        def p_src_fn(p_rows, p_smp_rows):
            def f(c0, W):
                if c0 >= NP_OWN:
                    return [(p_smp_rows, NS, 0)]
                return [(p_rows[c0 + s0:c0 + s0 + 128, :], 128, s0) for s0 in range(0, W, 128)]
            return f

        ar.phase()
        pe.prepass(ar, peer_u[0], peer_v[0], uT_scr[0], v_scr[0], b_scr[0])
        if stop_after <= 0:
            return dump_and_finish()
        ar.phase(); kb.set_xtok(ar)
        load_x(x_ctx, 0, 2048)
        conv_phase(kb, ar, R, b_R, vecT, b_vec, hT, b_hT, ptiles, w_in, w_out, None, None, b_tab, tail_save=(gtail[:], b_gtail))
        if stop_after <= 1:
            return dump_and_finish()
        pe.run_phase(ar, 0, peer_wq[0], sub_keys[0], vecT[:, :, V_FFN + 0], b_vec, ptiles, hT, b_hT, uT_scr[0], v_scr[0], b_scr[0])
        if stop_after <= 2:
            return dump_and_finish()
        ple_phase(kb, ar, R, b_R, vecT, b_vec, 0, hT, b_hT, ptiles, ple_wg[0], ple_wp[0], p_src_fn(p_ctx0, None))
        if stop_after <= 3:
            return dump_and_finish()
        qkv_phase(kb, ar, R, b_R, vecT, b_vec, hT, b_hT, [(c0, W, c0) for (c0, W) in ptiles], wqkv, KT_ctx, V_ctx, b_kv)
        if stop_after <= 4:
            return dump_and_finish()
        ar.phase(); kb.set_xtok(ar)
        load_x(x_own, 0, 2048)
        load_x(x_smp, NP_OWN, NS)
        conv_phase(kb, ar, R, b_R, vecT, b_vec, hT, b_hT, ptiles, w_in, w_out, (gtail[:], b_gtail), flags[:, 0:1], b_tab,
                   smp=dict(c0=NP_OWN, state_conv=state_conv, out_state=conv_smp, b_out=b_out), tail_out=(conv_tail, b_out))
        if stop_after <= 5:
            return dump_and_finish()
        pe.run_phase(ar, 0, peer_wq[0], sub_keys[0], vecT[:, :, V_FFN + 0], b_vec, ptiles + stiles, hT, b_hT, uT_scr[0], v_scr[0], b_scr[0])
        ple_phase(kb, ar, R, b_R, vecT, b_vec, 0, hT, b_hT, ptiles + stiles, ple_wg[0], ple_wp[0], p_src_fn(p_own[0], p_smp[0]))
        if stop_after <= 6:
            return dump_and_finish()
        ar.phase()
        pe.prepass(ar, peer_u[1], peer_v[1], uT_scr[1], v_scr[1], b_scr[1])
        Rf = R[:].bitcast(F32)
        smp = dict(c0=NP_OWN, qtok=Rf[:, 12352:13376], ktok=Rf[:, 13376:14400], vtok=Rf[:, 14400:15424],
                   b_q=b_R, b_k=b_R, b_v=b_R, k_out=k_smp, v_out=v_smp, b_out=b_out)
        qkv_phase(kb, ar, R, b_R, vecT, b_vec, hT, b_hT, [(c0, W, c0) for (c0, W) in ptiles] + [(NP_OWN, NS, None)], wqkv, KT_own, V_own, b_kv,
                  QT_scr=QT_scr, k_out=k_own, v_out=v_own, b_out=b_out, smp=smp)
        if stop_after <= 7:
            return dump_and_finish()
        OT = R[:, 0:KC * NCOL].rearrange('p (k t) -> p k t', k=KC)
        b_OT = b_R
        attn_prompt(kb, ar, OT, b_OT, KT_ctx, KT_own, V_ctx, V_own, QT_scr, b_kv, band_d, c31, b_tab, vmask, notown)
        if stop_after <= 8:
            return dump_and_finish()
        attn_sample(kb, ar, OT, b_OT, smp, cache_k, cache_v, page_table, sbias[:], b_tab, o_scr, b_oscr)
        wo_view = R[:, 16512:16512 + KC * 1024].rearrange('p (k n) -> p k n', k=KC)
        wo_phase(kb, ar, R, b_R, wo_view, OT, b_OT, hT, b_hT, ptiles + stiles, wo)
        if stop_after <= 9:
            return dump_and_finish()
        pe.run_phase(ar, 1, peer_wq[1], sub_keys[1], vecT[:, :, V_FFN + 1], b_vec, ptiles + stiles, hT, b_hT, uT_scr[1], v_scr[1], b_scr[1])
        ple_phase(kb, ar, R, b_R, vecT, b_vec, 1, hT, b_hT, ptiles + stiles, ple_wg[1], ple_wp[1], p_src_fn(p_own[1], p_smp[1]))
        if stop_after <= 10:
            return dump_and_finish()
        final_phase(kb, ar, vecT, b_vec, hT, b_hT, [(c0, W, y_own[c0:c0 + W, :]) for (c0, W) in ptiles] + [(NP_OWN, NS, y_smp)], b_out)
        return dump_and_finish()


def _t5_bucket(d):
    d = np.asarray(d, np.int64)
    df = np.maximum(d, 1).astype(np.float32)
    large = 16 + (np.log(df / np.float32(16.0)) / np.float32(np.log(8.0)) * np.float32(16.0)).astype(np.int32)
    large = np.minimum(large, 31)
    return np.where(d < 16, d, large).astype(np.int64)


def make_core_inputs(c, inp):
    seq, half = c // 2, c % 2
    f32 = np.float32
    s0 = NS * c
    xp = inp['x_prompt']
    m = {}
    m['x_ctx'] = np.ascontiguousarray(xp[seq, 0:2048])
    m['x_own'] = np.ascontiguousarray(xp[seq, half * 2048:(half + 1) * 2048])
    m['x_smp'] = np.ascontiguousarray(inp['x_sample'][s0:s0 + NS, 0])
    pp = inp['p_prompt']
    m['p_ctx0'] = np.ascontiguousarray(pp[0, seq, 0:2048])
    m['p_own'] = np.ascontiguousarray(pp[:, seq, half * 2048:(half + 1) * 2048])
    m['p_smp'] = np.ascontiguousarray(inp['p_sample'][:, s0:s0 + NS, 0])
    m['state_conv'] = np.ascontiguousarray(inp['state_conv'][0, s0:s0 + NS])
    m['cache_k'] = inp['cache_k'].reshape(-1, 1024)
    m['cache_v'] = inp['cache_v'].reshape(-1, 1024)
    m['page_table'] = np.ascontiguousarray(inp['page_table'][s0:s0 + NS]).astype(np.int32)
    vecs = np.zeros((NVEC, 1024), f32)
    vecs[V_MIX:V_MIX + 2] = inp['norm_mix_g']; vecs[V_FFN:V_FFN + 2] = inp['norm_ffn_g']; vecs[V_PLE:V_PLE + 2] = inp['norm_ple_g']
    vecs[V_FIN] = inp['norm_final_g']
    vecs[V_BIN:V_BIN + 2] = inp['conv_b_in'][0].reshape(2, 1024)
    vecs[V_DWW:V_DWW + 31] = inp['conv_dw_w'][0]
    vecs[V_DWB] = inp['conv_dw_b'][0]; vecs[V_LNG] = inp['conv_ln_g'][0]; vecs[V_LNB] = inp['conv_ln_b'][0]; vecs[V_BOUT] = inp['conv_b_out'][0]
    m['vecs'] = vecs
    m['w_in'] = inp['conv_w_in'][0]; m['w_out'] = inp['conv_w_out'][0]
    m['wqkv'] = inp['attn_w_qkv'][0]; m['wo'] = inp['attn_w_o'][0]
    m['peer_wq'] = inp['peer_w_q']; m['sub_keys'] = inp['peer_sub_keys']
    m['peer_u'] = inp['peer_u']; m['peer_v'] = inp['peer_v']
    m['ple_wp'] = inp['ple_w_proj']; m['ple_wg'] = inp['ple_w_gate']
    m['flags'] = np.full((128, 1), float(half), f32)
    vm = np.zeros((16, 16), f32); no = np.ones((16, 16), f32)
    for qt in range(16):
        own = 8 + qt // 2
        for b in range(16):
            valid = (b < own) and (b >= 8 or half == 1)
            vm[qt, b] = 0.0 if valid else -2e30
        no[qt, own] = 0.0
    m['vmask'] = np.broadcast_to(vm, (128, 16, 16)).copy()
    m['notown'] = np.broadcast_to(no, (128, 16, 16)).copy()
    rb = inp['rel_bias']
    m['c31'] = np.broadcast_to(rb[31], (128, 16)).copy()
    sb = np.zeros((128, 17, 16), f32)
    p = np.arange(128)
    for pg in range(16):
        dist = 2048 - (pg * 128 + p)
        sb[:, pg, :] = rb[_t5_bucket(dist)]
    sb[:, 16, :] = -30000.0
    sb[0, 16, :] = rb[0]
    m['sbias'] = sb
    k = np.arange(128)[:, None]
    mm = np.arange(1024)[None, :] - 384
    dd = mm - k
    bk = _t5_bucket(np.maximum(dd, 0))
    band = np.empty((16, 128, 1024), f32)
    for h in range(16):
        band[h] = np.where(dd < 0, f32(-30000.0), rb[bk, h])
    m['band'] = band
    return m


_NC_CACHE = {}


def kernel(**inp):
    inp = {k: np.asarray(v) for k, v in inp.items()}
    if 'nc' not in _NC_CACHE:
        _NC_CACHE['nc'] = build_program()
    nc = _NC_CACHE['nc']
    in_maps = [make_core_inputs(c, inp) for c in range(8)]
    res = run_bass_kernel_spmd(nc, in_maps, core_ids=list(range(8)))
    r = res.results
    f32 = np.float32
    y_prompt = np.zeros((4, 4096, 1024), f32); y_sample = np.zeros((128, 1, 1024), f32)
    ncp = np.zeros((1, 4, 30, 1024), f32); ncs = np.zeros((1, 128, 30, 1024), f32)
    kp = np.zeros((1, 4, 4096, 16, 64), f32); vp = np.zeros((1, 4, 4096, 16, 64), f32)
    ks = np.zeros((1, 128, 1, 16, 64), f32); vs = np.zeros((1, 128, 1, 16, 64), f32)
    for c in range(8):
        seq, half = c // 2, c % 2
        sl = slice(half * 2048, (half + 1) * 2048)
        ss = slice(NS * c, NS * (c + 1))
        y_prompt[seq, sl] = r[c]['y_own']
        y_sample[ss, 0] = r[c]['y_smp']
        if half == 1:
            ncp[0, seq] = r[c]['conv_tail']
        ncs[0, ss] = r[c]['conv_smp']
        kp[0, seq, sl] = r[c]['k_own'].reshape(2048, 16, 64)
        vp[0, seq, sl] = r[c]['v_own'].reshape(2048, 16, 64)
        ks[0, ss, 0] = r[c]['k_smp'].reshape(NS, 16, 64)
        vs[0, ss, 0] = r[c]['v_smp'].reshape(NS, 16, 64)
    return (y_prompt, y_sample, ncp, ncs, kp, vp, ks, vs)
```

```python
import contextlib
import numpy as np
import concourse.bass as bass
import concourse.mybir as mybir
from concourse.bass_utils import run_bass_kernel_spmd

F32 = mybir.dt.float32
BF16 = mybir.dt.bfloat16
I32 = mybir.dt.int32
U32 = mybir.dt.uint32
AF = mybir.ActivationFunctionType
ALU = mybir.AluOpType
AX = mybir.AxisListType

ENGS = ['pe', 'act', 'dve', 'pool', 'sp']
N_DMA_SEM = 8


class Buf:
    __slots__ = ('name', 'last_w', 'readers')

    def __init__(self, name='b'):
        self.name = name
        self.last_w = None
        self.readers = []


class _Rec:
    def __init__(self):
        self.call = None

    def __getattr__(self, name):
        def f(*a, **k):
            self.call = (name, a, k)
            return self
        return f


def _record(fn):
    r = _Rec()
    fn(r)
    assert r.call is not None
    return r.call


class Sched:
    def __init__(self, nc, stack):
        self.nc = nc
        self.ops = {e: [] for e in ENGS}
        self.cnt = {e: 0 for e in ENGS}
        self.seen = {e: {} for e in ENGS}
        self.esem = {e: stack.enter_context(nc.semaphore('es_' + e)) for e in ENGS}
        self.dsem = {q: [stack.enter_context(nc.semaphore('ds_%s%d' % (q, i))) for i in range(N_DMA_SEM)]
                     for q in ('sp', 'pool', 'act')}
        self.dcnt = {q: 0 for q in ('sp', 'pool', 'act')}
        self.semobj = {}
        for e in ENGS:
            self.semobj[('e', e)] = self.esem[e]
        for q in self.dsem:
            for i, s in enumerate(self.dsem[q]):
                self.semobj[('d', q, i)] = s
        self.final_tokens = []

    def _collect(self, eng, reads, writes, extra=()):
        need = {}

        def add(tok):
            if tok is None:
                return
            k, v = tok
            if v > need.get(k, 0):
                need[k] = v
        for b in reads:
            add(b.last_w)
        for b in writes:
            add(b.last_w)
            for r in b.readers:
                add(r)
        for t in extra:
            add(t)
        waits = []
        seen = self.seen[eng]
        for k, v in need.items():
            if eng == 'pe' and k == ('e', 'pe'):
                continue
            if v > seen.get(k, 0):
                waits.append((k, v))
                seen[k] = v
        return waits

    def _post(self, tok, reads, writes):
        for b in reads:
            if len(b.readers) > 64:
                mx = {}
                for k, v in b.readers:
                    if v > mx.get(k, 0):
                        mx[k] = v
                b.readers = list(mx.items())
            b.readers.append(tok)
        for b in writes:
            b.last_w = tok
            b.readers = []

    def op(self, eng, fn, reads=(), writes=()):
        waits = self._collect(eng, reads, writes)
        idx = self.cnt[eng]
        self.cnt[eng] += 1
        tok = (('e', eng), idx + 1)
        self.ops[eng].append((waits, _record(fn), (self.esem[eng], 1)))
        self._post(tok, reads, writes)
        return tok

    def dma(self, q, fn, reads=(), writes=(), final=False):
        k = self.dcnt[q]
        self.dcnt[q] += 1
        slot = k % N_DMA_SEM
        rnd = k // N_DMA_SEM
        key = ('d', q, slot)
        extra = [(key, 16 * rnd)] if rnd > 0 else []
        waits = self._collect(q, reads, writes, extra)
        tok = (key, 16 * (rnd + 1))
        self.ops[q].append((waits, _record(fn), (self.dsem[q][slot], 16)))
        self._post(tok, reads, writes)
        if final:
            self.final_tokens.append(tok)
        return tok

    def merge(self, olds, name='m'):
        nb = Buf(name)
        for o in olds:
            if o.last_w is not None:
                nb.readers.append(o.last_w)
            nb.readers.extend(o.readers)
        return nb

    def emit(self):
        nc = self.nc
        fin = {}
        for k, v in self.final_tokens:
            fin[k] = max(fin.get(k, 0), v)
        for e in ENGS:
            if e != 'sp' and self.cnt[e] > 0:
                fin[('e', e)] = self.cnt[e]
        for q in self.dsem:
            kk = self.dcnt[q]
            for slot in range(N_DMA_SEM):
                n = (kk - slot + N_DMA_SEM - 1) // N_DMA_SEM if kk > slot else 0
                if n > 0:
                    fin[('d', q, slot)] = max(fin.get(('d', q, slot), 0), 16 * n)
        final_waits = list(fin.items())
        semobj = self.semobj
        ops = self.ops

        def replay(eobj, lst, extra_waits=()):
            for waits, fn, inc in lst:
                for k, v in waits:
                    eobj.wait_ge(semobj[k], v)
                ins = getattr(eobj, fn[0])(*fn[1], **fn[2])
                ins.then_inc(inc[0], inc[1])
            for k, v in extra_waits:
                eobj.wait_ge(semobj[k], v)

        with nc.Block() as block:
            @block.sync
            def _(e):
                replay(e, ops['sp'], final_waits)

            @block.tensor
            def _(e):
                replay(e, ops['pe'])

            @block.scalar
            def _(e):
                replay(e, ops['act'])

            @block.vector
            def _(e):
                replay(e, ops['dve'])

            @block.gpsimd
            def _(e):
                replay(e, ops['pool'])


def mkap(base, off, dims):
    p = base.ap[0]
    return bass.AP(tensor=base.tensor, offset=base.offset + off, ap=[[p[0], p[1]]] + [list(d) for d in dims])


D = 1024
KC = 8
NEXP = 16384
NCH = 128
PH = 8
TT = 256


NP_OWN = 2048
NS = 16
NCOL = NP_OWN + NS
ARENA_F32 = 17900


class Arena:
    def __init__(self, kb, nwords):
        self.kb = kb
        self.t = kb.sb('arena', [128, nwords], F32)
        self.n = nwords
        self.off = 0
        self.cur = []
        self.prev = []

    def phase(self):
        allb = self.cur + self.prev
        self.prev = [self.kb.S.merge(allb, 'arena_prev')] if allb else []
        self.cur = []
        self.off = 0

    def alloc(self, shape, dt, name='a'):
        nfree = 1
        for d in shape[1:]:
            nfree *= d
        bpe = 4 if dt in (F32, I32, U32) else 2
        words = (nfree * bpe + 3) // 4
        words = (words + 7) // 8 * 8
        assert self.off + words <= self.n, ('arena overflow', name, self.off, words, self.n)
        base = self.t[:, self.off:self.off + words]
        self.off += words
        if bpe == 2:
            ap = base.bitcast(dt)[:, 0:nfree]
        elif dt != F32:
            ap = base.bitcast(dt)[:, 0:nfree]
        else:
            ap = base[:, 0:nfree]
        if len(shape) > 2:
            names = ' '.join('d%d' % i for i in range(len(shape) - 1))
            kw = {'d%d' % i: shape[i + 1] for i in range(len(shape) - 1)}
            ap = ap.rearrange('p (%s) -> p %s' % (names, names), **kw)
        b = self.kb.S.merge(self.prev, name) if self.prev else Buf(name)
        self.cur.append(b)
        return ap, b


class KB:
    def __init__(self, nc, st):
        self.nc = nc
        self.st = st
        self.S = Sched(nc, st)
        S = self.S
        self.ps = [st.enter_context(nc.psum_tensor('psb%d' % i, [128, 512], F32)) for i in range(8)]
        self.b_ps = [Buf('ps%d' % i) for i in range(8)]
        self.ident_f = self.sb('ident_f', [128, 128], F32)
        self.ident_b = self.sb('ident_b', [128, 128], BF16)
        self.iota_f = self.sb('iota_f', [128, 128], F32)
        self.iota_p = self.sb('iota_p', [128, 1], F32)
        self.ones_f = self.sb('ones_f', [128, 128], F32)
        self.ones_b = self.sb('ones_b', [128, 128], BF16)
        self.b_const = Buf('const')
        tmp_i = self.sb('tmp_iota_i', [128, 128], I32)
        tmp_f = self.sb('tmp_iota_f', [128, 128], F32)
        bt = Buf('tmpi')
        S.op('pool', lambda e: e.iota(tmp_i[:], pattern=[[1, 128]], base=0, channel_multiplier=0), writes=[bt])
        S.op('dve', lambda e: e.tensor_copy(out=self.iota_f[:], in_=tmp_i[:]), reads=[bt], writes=[self.b_const])
        S.op('pool', lambda e: e.iota(tmp_i[:], pattern=[[1, 128]], base=0, channel_multiplier=-1), reads=[], writes=[bt])
        bt2 = Buf('tmpf')
        S.op('dve', lambda e: e.tensor_copy(out=tmp_f[:], in_=tmp_i[:]), reads=[bt], writes=[bt2])
        S.op('dve', lambda e: e.tensor_single_scalar(out=self.ident_f[:], in_=tmp_f[:], scalar=0.0, op=ALU.is_equal),
             reads=[bt2], writes=[self.b_const])
        S.op('dve', lambda e: e.tensor_copy(out=self.ident_b[:], in_=self.ident_f[:]), reads=[self.b_const], writes=[self.b_const])
        S.op('pool', lambda e: e.memset(self.ones_f[:], 1.0), writes=[self.b_const])
        S.op('pool', lambda e: e.memset(self.ones_b[:], 1.0), writes=[self.b_const])
        S.op('pool', lambda e: e.iota(tmp_i[:, 0:1], pattern=[[1, 1]], base=0, channel_multiplier=1), reads=[], writes=[bt])
        S.op('dve', lambda e: e.tensor_copy(out=self.iota_p[:], in_=tmp_i[:, 0:1]), reads=[bt], writes=[self.b_const])
        self.n_xtok = 0
        self.rr = 0

    def set_xtok(self, ar):
        self.xtok = []
        self.b_xtok = []
        for i in range(2):
            a, b = ar.alloc([128, 1024], F32, 'xtok')
            self.xtok.append(a)
            self.b_xtok.append(b)

    def sb(self, name, shape, dt):
        return self.st.enter_context(self.nc.sbuf_tensor(name, shape, dt))

    def bank(self):
        b = self.rr % 5
        self.rr += 1
        return b

    def load_T(self, x_rows, P, dst_fn, b_dst, ncols=1024, q='sp', b_src=None):
        S = self.S
        k = self.n_xtok % 2
        self.n_xtok += 1
        xt = self.xtok[k]
        S.dma(q, lambda e: e.dma_start(out=xt[0:P, 0:ncols], in_=x_rows), reads=([b_src] if b_src is not None else []), writes=[self.b_xtok[k]])
        nkc = ncols // 128
        for g0 in range(0, nkc, 4):
            nk = min(4, nkc - g0)
            pbi = 5 + ((g0 // 4) % 2)
            pb = self.ps[pbi]
            for j in range(nk):
                kc = g0 + j
                S.op('pe', lambda e: e.transpose(out=pb[:, j * 128:j * 128 + P], in_=xt[0:P, kc * 128:(kc + 1) * 128],
                                                 identity=self.ident_f[0:P, 0:P]),
                     reads=[self.b_xtok[k], self.b_const], writes=[self.b_ps[pbi]])
            src = pb[:, 0:nk * 128].rearrange('p (j t) -> p j t', j=nk)[:, :, 0:P]
            S.op('act', lambda e: e.activation(out=dst_fn(g0, nk), in_=src, func=AF.Copy),
                 reads=[self.b_ps[pbi]], writes=[b_dst])

    def store_T(self, src_fn, b_src, P, out_rows, b_out, final=True, q='sp'):
        S = self.S
        k = self.n_xtok % 2
        self.n_xtok += 1
        xt = self.xtok[k]
        for g0 in range(0, KC, 4):
            pbi = 5 + ((g0 // 4) % 2)
            pb = self.ps[pbi]
            for j in range(4):
                kc = g0 + j
                S.op('pe', lambda e: e.transpose(out=pb[0:P, j * 128:(j + 1) * 128], in_=src_fn(kc), identity=self.ident_f[:]),
                     reads=[b_src, self.b_const], writes=[self.b_ps[pbi]])
            S.op('act', lambda e: e.activation(out=xt[0:P, g0 * 128:(g0 + 4) * 128], in_=pb[0:P, :], func=AF.Copy),
                 reads=[self.b_ps[pbi]], writes=[self.b_xtok[k]])
        S.dma(q, lambda e: e.dma_start(out=out_rows, in_=xt[0:P, :]), reads=[self.b_xtok[k]], writes=[b_out], final=final)

    def rmsnorm_to(self, ar, hT, b_hT, c0, W, gvec, b_gvec, out_ap, b_out, tmp=None):
        S = self.S
        if tmp is None:
            sq, b_sq = ar.alloc([128, KC, TT], F32, 'sq')
            rstd, b_rstd = ar.alloc([128, TT], F32, 'rstd')
        else:
            sq, b_sq, rstd, b_rstd = tmp
        S.op('act', lambda e: e.activation(out=sq[:, :, 0:W], in_=hT[:, :, c0:c0 + W], func=AF.Square), reads=[b_hT], writes=[b_sq])
        pb = self.ps[7]
        for kc in range(KC):
            S.op('pe', lambda e: e.matmul(pb[:, 0:W], lhsT=self.ones_f[:], rhs=sq[:, kc, 0:W], start=(kc == 0), stop=(kc == KC - 1)),
                 reads=[b_sq, self.b_const], writes=[self.b_ps[7]])
        S.op('act', lambda e: e.activation(out=rstd[:, 0:W], in_=pb[:, 0:W], func=AF.Sqrt, scale=1.0 / D, bias=1e-6),
             reads=[self.b_ps[7]], writes=[b_rstd])
        S.op('dve', lambda e: e.reciprocal(out=rstd[:, 0:W], in_=rstd[:, 0:W]), reads=[b_rstd], writes=[b_rstd])
        for kc in range(KC):
            S.op('dve', lambda e: e.scalar_tensor_tensor(out=out_ap[:, kc, 0:W], in0=hT[:, kc, c0:c0 + W], scalar=gvec[:, kc:kc + 1],
                                                         in1=rstd[:, 0:W], op0=ALU.mult, op1=ALU.mult),
                 reads=[b_hT, b_rstd, b_gvec], writes=[b_out])

    def lin(self, ps_ap, b_psb, Wsb, b_W, n0, nn, xin, b_x, W, nkc=KC):
        S = self.S
        for kc in range(nkc):
            S.op('pe', lambda e: e.matmul(ps_ap, lhsT=Wsb[:, kc, n0:n0 + nn], rhs=xin[:, kc, 0:W], start=(kc == 0), stop=(kc == nkc - 1)),
                 reads=[b_W, b_x], writes=[b_psb])

    def load_w(self, dst, b_dst, w_dram, nkc=KC):
        self.S.dma('pool', lambda e: e.dma_start(out=dst, in_=w_dram.rearrange('(kc p) n -> p kc n', p=128)), writes=[b_dst])


class Peer:
    def __init__(self, kb, R):
        self.kb = kb
        self.Gt = R
        self.b_Gt = Buf('Gt')
        self.cnt_ab = 0
        self.b_ps4 = [Buf('ps4a'), Buf('ps4b')]

    def alloc(self, ar):
        a = ar.alloc
        self.wq, self.b_wq = a([128, KC, 1024], BF16, 'wq')
        self.skT, self.b_skT = a([128, PH, 2, 128], BF16, 'skT')
        self.xnT, self.b_xnT = a([128, KC, TT], BF16, 'xnT')
        self.scr8, self.b_scr8 = a([128, 2048], F32, 'scr8')
        self.sq = self.scr8.rearrange('p (k t) -> p k t', k=KC); self.b_sq = self.b_scr8
        self.rstd, self.b_rstd = a([128, TT], F32, 'rstd')
        self.qT, self.b_qT = a([128, PH, TT], BF16, 'qT')
        self.stop, self.b_stop = a([128, PH, 2, 16], F32, 'stop')
        self.itop, self.b_itop = a([128, PH, 2, 16], U32, 'itop')
        self.itopf, self.b_itopf = a([128, PH, 2, 16], F32, 'itopf')
        self.smod, self.b_smod = a([128, 256], F32, 'smod')
        self.cand, self.b_cand = a([128, 256], F32, 'cand')
        self.ctop, self.b_ctop = a([128, PH, 16], F32, 'ctop')
        self.cpos, self.b_cpos = a([128, PH, 16], U32, 'cpos')
        self.ak, self.b_ab = a([128, 128], U32, 'ak')
        self.bk, _ = a([128, 128], U32, 'bk')
        self.akf, _ = a([128, 128], F32, 'akf')
        self.bkf, _ = a([128, 128], F32, 'bkf')
        self.negmax, self.b_negmax = a([128, PH], F32, 'negmax')
        self.Z, self.b_Z = a([128, PH], F32, 'Z')
        self.ee, self.b_ee = a([128, PH, 16], F32, 'ee')
        self.g, self.b_g = a([128, 128], F32, 'g')
        self.i1s, self.b_i1s = a([128, 128], F32, 'i1s')
        self.i2s, self.b_i2s = a([128, 128], F32, 'i2s')
        self.gT, self.b_gT = a([128, TT], F32, 'gT')
        self.i1T, self.b_i1T = a([128, TT], F32, 'i1T')
        self.i2T, self.b_i2T = a([128, TT], F32, 'i2T')
        self.NAB = 4
        self.A, self.b_A, self.B, self.b_B = [], [], [], []
        for i in range(self.NAB):
            x, b = a([128, 128], BF16, 'A'); self.A.append(x); self.b_A.append(b)
            x, b = a([128, 128], BF16, 'B'); self.B.append(x); self.b_B.append(b)
        self.uTb, self.b_uTb, self.vb, self.b_vb, self.ge, self.b_ge, self.wT, self.b_wT = [], [], [], [], [], [], [], []
        for i in range(2):
            x, b = a([128, KC, 128], BF16, 'uTb'); self.uTb.append(x); self.b_uTb.append(b)
            x, b = a([128, 1024], BF16, 'vb'); self.vb.append(x); self.b_vb.append(b)
        for i in range(2):
            x, b = a([128, TT], BF16, 'ge'); self.ge.append(x); self.b_ge.append(b)
            x, b = a([128, TT], BF16, 'wT'); self.wT.append(x); self.b_wT.append(b)
        self.iota16, b = a([128, 16], F32, 'iota16')
        kb = self.kb
        kb.S.op('dve', lambda e: e.tensor_copy(out=self.iota16[:], in_=kb.iota_f[:, 0:16]), reads=[kb.b_const], writes=[b])
        self.b_iota16 = b

    def prepass(self, ar, u_l, v_l, uT_scr, v_scr, b_scr):
        kb = self.kb; S = kb.S
        ps = kb.ps
        unat, b_unat, uTs, b_uTs = [], [], [], []
        for i in range(2):
            x, b = ar.alloc([128, 1024], BF16, 'unat'); unat.append(x); b_unat.append(b)
            x, b = ar.alloc([128, 1024], BF16, 'uTs'); uTs.append(x); b_uTs.append(b)
        for i in range(NCH):
            k = i % 2
            un, ut = unat[k], uTs[k]
            S.dma('pool', lambda e: e.dma_start(out=un, in_=u_l[i * 128:(i + 1) * 128, :]), writes=[b_unat[k]])
            pb = ps[5 + k]
            psv = pb[:].bitcast(BF16)
            for kc in range(KC):
                S.op('pe', lambda e: e.transpose(out=psv[:, kc * 128:(kc + 1) * 128], in_=un[:, kc * 128:(kc + 1) * 128], identity=kb.ident_b[:]),
                     reads=[b_unat[k], kb.b_const], writes=[kb.b_ps[5 + k]])
            if k == 0:
                S.op('act', lambda e: e.activation(out=ut, in_=psv, func=AF.Copy), reads=[kb.b_ps[5 + k]], writes=[b_uTs[k]])
            else:
                S.op('dve', lambda e: e.tensor_copy(out=ut, in_=psv), reads=[kb.b_ps[5 + k]], writes=[b_uTs[k]])
            S.dma('sp', lambda e: e.dma_start(out=uT_scr[i], in_=ut), reads=[b_uTs[k]], writes=[b_scr])
        for i in range(NCH):
            k = i % 2
            un = unat[k]
            S.dma('pool', lambda e: e.dma_start(out=un, in_=v_l[i * 128:(i + 1) * 128, :]), writes=[b_unat[k]])
            S.dma('sp', lambda e: e.dma_start(out=v_scr[i * 128:(i + 1) * 128, :], in_=un), reads=[b_unat[k]], writes=[b_scr])

    def load_layer(self, wq_l, sk_l):
        kb = self.kb; S = kb.S
        kb.load_w(self.wq, self.b_wq, wq_l)
        sknat = self.scr8[:, 0:1024].rearrange('p (h c) -> p h c', h=PH)
        for c in range(2):
            S.dma('sp', lambda e: e.dma_start(out=sknat[:, :, c * 64:(c + 1) * 64], in_=sk_l[:, c].rearrange('h n d -> n h d')),
                  writes=[self.b_scr8])
        S.op('pool', lambda e: e.memset(self.skT, 0.0), writes=[self.b_skT])
        for h in range(PH):
            pb = kb.ps[7]
            S.op('pe', lambda e: e.transpose(out=pb[:, 0:128], in_=sknat[:, h, :], identity=kb.ident_f[:]),
                 reads=[self.b_scr8, kb.b_const], writes=[kb.b_ps[7]])
            S.op('act', lambda e: e.activation(out=self.skT[0:64, h, 0, :], in_=pb[0:64, 0:128], func=AF.Copy),
                 reads=[kb.b_ps[7]], writes=[self.b_skT])
            S.op('act', lambda e: e.activation(out=self.skT[64:128, h, 1, :], in_=pb[64:128, 0:128], func=AF.Copy),
                 reads=[kb.b_ps[7]], writes=[self.b_skT])

    def top16(self, src_ap, b_src, n, top_ap, idx_ap, b_top, b_idx):
        S = self.kb.S
        P = src_ap.ap[0][1]
        smv = self.smod[0:P, 0:n]
        S.op('dve', lambda e: e.max(out=top_ap[:, 0:8], in_=src_ap), reads=[b_src], writes=[b_top])
        S.op('dve', lambda e: e.max_index(out=idx_ap[:, 0:8], in_max=top_ap[:, 0:8], in_values=src_ap), reads=[b_src, b_top], writes=[b_idx])
        S.op('dve', lambda e: e.match_replace(out=smv, in_to_replace=top_ap[:, 0:8], in_values=src_ap, imm_value=-1e30),
             reads=[b_src, b_top], writes=[self.b_smod])
        S.op('dve', lambda e: e.max(out=top_ap[:, 8:16], in_=smv), reads=[self.b_smod], writes=[b_top])
        S.op('dve', lambda e: e.max_index(out=idx_ap[:, 8:16], in_max=top_ap[:, 8:16], in_values=smv), reads=[self.b_smod, b_top], writes=[b_idx])

    def route_tile(self, W):
        kb = self.kb; S = kb.S
        ps = kb.ps; b_ps = kb.b_ps
        for h in range(PH):
            pbi = 5 + (h % 2)
            pb = ps[pbi]
            kb.lin(pb[:, 0:W], b_ps[pbi], self.wq, self.b_wq, h * 128, 128, self.xnT, self.b_xnT, W)
            S.op('act', lambda e: e.activation(out=self.qT[:, h, 0:W], in_=pb[:, 0:W], func=AF.Copy), reads=[b_ps[pbi]], writes=[self.b_qT])
        nsub = (W + 127) // 128
        for sub in range(nsub):
            P = min(128, W - sub * 128)
            t0 = sub * 128
            for h in range(PH):
                pbi = 5 + (h % 2)
                pb = ps[pbi]
                S.op('pe', lambda e: e.matmul(pb[0:P, 0:256], lhsT=self.qT[:, h, t0:t0 + P],
                                              rhs=self.skT[:, h, :, :].rearrange('p c n -> p (c n)'), start=True, stop=True),
                     reads=[self.b_qT, self.b_skT], writes=[b_ps[pbi]])
                for c in range(2):
                    self.top16(pb[0:P, c * 128:(c + 1) * 128], b_ps[pbi], 128, self.stop[0:P, h, c, :], self.itop[0:P, h, c, :],
                               self.b_stop, self.b_itop)
            S.op('dve', lambda e: e.tensor_copy(out=self.itopf[0:P], in_=self.itop[0:P]), reads=[self.b_itop], writes=[self.b_itopf])
            for h in range(PH):
                base = self.stop[0:P, h, 0, :]
                S.op('dve', lambda e: e.tensor_tensor(out=self.cand[0:P, :].rearrange('p (a b) -> p a b', a=16),
                                                      in0=mkap(base, 0, [[1, 16], [0, 16]]), in1=mkap(base, 16, [[0, 16], [1, 16]]), op=ALU.add),
                     reads=[self.b_stop], writes=[self.b_cand])
                self.top16(self.cand[0:P, :], self.b_cand, 256, self.ctop[0:P, h, :], self.cpos[0:P, h, :], self.b_ctop, self.b_cpos)
            S.op('dve', lambda e: e.tensor_single_scalar(out=self.negmax[0:P], in_=self.ctop[0:P, :, 0], scalar=-1.0, op=ALU.mult),
                 reads=[self.b_ctop], writes=[self.b_negmax])
            for h in range(PH):
                S.op('act', lambda e: e.activation(out=self.ee[0:P, h, :], in_=self.ctop[0:P, h, :], func=AF.Exp,
                                                   bias=self.negmax[0:P, h:h + 1], accum_out=self.Z[0:P, h:h + 1]),
                     reads=[self.b_ctop, self.b_negmax], writes=[self.b_ee, self.b_Z])
            S.op('dve', lambda e: e.reciprocal(out=self.Z[0:P], in_=self.Z[0:P]), reads=[self.b_Z], writes=[self.b_Z])
            S.op('dve', lambda e: e.tensor_tensor(out=self.g[0:P].rearrange('p (h k) -> p h k', h=PH), in0=self.ee[0:P],
                                                  in1=mkap(self.Z[0:P], 0, [[1, PH], [0, 16]]), op=ALU.mult),
                 reads=[self.b_ee, self.b_Z], writes=[self.b_g])
            cposf = self.cpos[0:P].rearrange('p h k -> p (h k)')
            S.op('dve', lambda e: e.tensor_single_scalar(out=self.ak[0:P], in_=cposf, scalar=4, op=ALU.logical_shift_right),
                 reads=[self.b_cpos], writes=[self.b_ab])
            S.op('dve', lambda e: e.tensor_single_scalar(out=self.bk[0:P], in_=cposf, scalar=15, op=ALU.bitwise_and),
                 reads=[self.b_cpos], writes=[self.b_ab])
            S.op('dve', lambda e: e.tensor_copy(out=self.akf[0:P], in_=self.ak[0:P]), reads=[self.b_ab], writes=[self.b_ab])
            S.op('dve', lambda e: e.tensor_copy(out=self.bkf[0:P], in_=self.bk[0:P]), reads=[self.b_ab], writes=[self.b_ab])
            HH = 4
            oh = self.scr8[0:P, 0:1024]
            oh2 = self.scr8[0:P, 1024:2048]
            for (kf, cc, outs, b_outs) in ((self.akf, 0, self.i1s, self.b_i1s), (self.bkf, 1, self.i2s, self.b_i2s)):
                for hh in range(0, PH, HH):
                    S.op('dve', lambda e: e.tensor_tensor(out=oh.rearrange('p (m a) -> p m a', a=16),
                                                          in0=mkap(kf[0:P, hh * 16:hh * 16 + 1], 0, [[1, HH * 16], [0, 16]]),
                                                          in1=mkap(self.iota16[0:P], 0, [[0, HH * 16], [1, 16]]), op=ALU.is_equal),
                         reads=[self.b_ab, self.b_iota16], writes=[self.b_scr8])
                    itb = self.itopf[0:P, hh, cc, :]
                    S.op('pool', lambda e: e.tensor_tensor(out=oh2.rearrange('p (h k a) -> p h k a', h=HH, k=16),
                                                           in0=oh.rearrange('p (h k a) -> p h k a', h=HH, k=16),
                                                           in1=mkap(itb, 0, [[32, HH], [0, 16], [1, 16]]), op=ALU.mult),
                         reads=[self.b_scr8, self.b_itopf], writes=[self.b_scr8])
                    S.op('dve', lambda e: e.tensor_reduce(out=outs[0:P, hh * 16:(hh + HH) * 16], in_=oh2.rearrange('p (m a) -> p m a', a=16),
                                                          axis=AX.X, op=ALU.add),
                         reads=[self.b_scr8], writes=[b_outs])
            for (src, b_src, dst, b_dst) in ((self.g, self.b_g, self.gT, self.b_gT), (self.i1s, self.b_i1s, self.i1T, self.b_i1T),
                                            (self.i2s, self.b_i2s, self.i2T, self.b_i2T)):
                pb = ps[7]
                S.op('pe', lambda e: e.transpose(out=pb[:, 0:P], in_=src[0:P, :], identity=kb.ident_f[0:P, 0:P]),
                     reads=[b_src, kb.b_const], writes=[b_ps[7]])
                S.op('act', lambda e: e.activation(out=dst[:, t0:t0 + P], in_=pb[:, 0:P], func=AF.Copy), reads=[b_ps[7]], writes=[b_dst])
        Gt = self.Gt
        GW = TT
        for t in range(W):
            k = self.cnt_ab % self.NAB
            self.cnt_ab += 1
            A, Bm = self.A[k], self.B[k]
            S.op('dve', lambda e: e.tensor_scalar(out=A, in0=kb.iota_f[:], scalar1=self.i1T[:, t:t + 1], scalar2=self.gT[:, t:t + 1],
                                                  op0=ALU.is_equal, op1=ALU.mult),
                 reads=[kb.b_const, self.b_i1T, self.b_gT], writes=[self.b_A[k]])
            S.op('pool', lambda e: e.tensor_scalar(out=Bm, in0=kb.iota_f[:], scalar1=self.i2T[:, t:t + 1], scalar2=None, op0=ALU.is_equal),
                 reads=[kb.b_const, self.b_i2T], writes=[self.b_B[k]])
            grp = t // 4
            pbi = 5 + (grp % 2)
            pb = ps[pbi]
            slot = t % 4
            S.op('pe', lambda e: e.matmul(pb[:, slot * 128:(slot + 1) * 128], lhsT=Bm, rhs=A, start=True, stop=True),
                 reads=[self.b_A[k], self.b_B[k]], writes=[b_ps[pbi]])
            if slot == 3 or t == W - 1:
                nt = slot + 1
                tb = t - slot
                outap = mkap(Gt[:, 0:1], tb, [[1, nt], [GW, 128]])
                S.op('act', lambda e: e.activation(out=outap, in_=pb[:, 0:nt * 128].rearrange('p (t i) -> p t i', t=nt), func=AF.Copy),
                     reads=[b_ps[pbi]], writes=[self.b_Gt])

    def expert_loop(self, W, uT_scr, v_scr, b_scr, hT, b_hT, c0):
        kb = self.kb; S = kb.S
        ps = kb.ps; b_ps = kb.b_ps
        GW = TT
        uT, vv, b_uT, b_vv = self.uTb, self.vb, self.b_uTb, self.b_vb
        al = []
        import os
        NB = int(os.environ.get('PEER_NB', '2'))
        PIPE = int(os.environ.get('PEER_PIPE', '1'))

        def u_stage(i):
            k = i % NB
            k2 = i % 2
            S.dma('sp', lambda e: e.dma_start(out=uT[k].rearrange('p kc e -> p (kc e)'), in_=uT_scr[i]), reads=[b_scr], writes=[b_uT[k]])
            S.dma(os.environ.get('PEER_VQ', 'sp'), lambda e: e.dma_start(out=vv[k], in_=v_scr[i * 128:(i + 1) * 128, :]), reads=[b_scr], writes=[b_vv[k]])
            pav = ps[4][:, k2 * 256:k2 * 256 + W]
            b_pa = self.b_ps4[k2]
            for kc in range(KC):
                S.op('pe', lambda e: e.matmul(pav, lhsT=uT[k][:, kc, :], rhs=self.xnT[:, kc, 0:W], start=(kc == 0), stop=(kc == KC - 1)),
                     reads=[b_uT[k], self.b_xnT], writes=[b_pa])
            S.op('act', lambda e: e.activation(out=self.ge[k2][:, 0:W], in_=pav, func=AF.Gelu), reads=[b_pa], writes=[self.b_ge[k2]])
            gsl = mkap(self.Gt[:, 0:1], i * GW, [[1, W]])
            S.op('dve', lambda e: e.tensor_tensor(out=self.wT[k2][:, 0:W], in0=self.ge[k2][:, 0:W], in1=gsl, op=ALU.mult),
                 reads=[self.b_ge[k2], self.b_Gt], writes=[self.b_wT[k2]])

        def v_stage(i):
            k = i % NB
            k2 = i % 2
            for dc in range(KC):
                po = ps[dc // 2][:, (dc % 2) * 256:(dc % 2) * 256 + W]
                S.op('pe', lambda e: e.matmul(po, lhsT=vv[k][:, dc * 128:(dc + 1) * 128], rhs=self.wT[k2][:, 0:W],
                                              start=(i == 0), stop=(i == NCH - 1)),
                     reads=[b_vv[k], self.b_wT[k2]], writes=[b_ps[dc // 2]])

        if PIPE:
            u_stage(0)
            for i in range(1, NCH):
                u_stage(i)
                v_stage(i - 1)
            v_stage(NCH - 1)
        else:
            for i in range(NCH):
                u_stage(i)
                v_stage(i)
        for dc in range(KC):
            po = ps[dc // 2][:, (dc % 2) * 256:(dc % 2) * 256 + W]
            S.op('dve', lambda e: e.tensor_tensor(out=hT[:, dc, c0:c0 + W], in0=hT[:, dc, c0:c0 + W], in1=po, op=ALU.add),
                 reads=[b_hT, b_ps[dc // 2]], writes=[b_hT])
        for b in al:
            if b.last_w is not None:
                self.b_scr8.readers.append(b.last_w)
            self.b_scr8.readers.extend(b.readers)

    def run_phase(self, ar, layer, wq_l, sk_l, gvec, b_vec, tiles, hT, b_hT, uT_scr, v_scr, b_scr):
        kb = self.kb
        ar.phase()
        self.alloc(ar)
        self.load_layer(wq_l, sk_l)
        for (c0, W) in tiles:
            kb.rmsnorm_to(ar, hT, b_hT, c0, W, gvec, b_vec, self.xnT, self.b_xnT, tmp=(self.sq, self.b_sq, self.rstd, self.b_rstd))
            self.route_tile(W)
            self.expert_loop(W, uT_scr, v_scr, b_scr, hT, b_hT, c0)


V_MIX = 0
V_FFN = 2
V_PLE = 4
V_FIN = 6
V_BIN = 7
V_DWW = 9
V_DWB = 40
V_LNG = 41
V_LNB = 42
V_BOUT = 43
NVEC = 64


def conv_phase(kb, ar, R, b_R, vecT, b_vec, hT, b_hT, tiles, w_in_d, w_out_d, halo_src, flag_ap, b_tab,
               smp=None, tail_save=None, tail_out=None):
    S = kb.S
    ar.phase()
    kb.set_xtok(ar)
    a = ar.alloc
    w_in = R[:, 0:KC * 2048].rearrange('p (k n) -> p k n', k=KC)
    w_out = R[:, KC * 2048:KC * 3072].rearrange('p (k n) -> p k n', k=KC)
    kb.load_w(w_in, b_R, w_in_d)
    kb.load_w(w_out, b_R, w_out_d)
    xnT, b_xnT = a([128, KC, TT], BF16, 'xnT')
    sq, b_sq = a([128, KC, TT], F32, 'sq')
    rstd, b_rstd = a([128, TT], F32, 'rstd')
    gbuf, b_gbuf = a([128, KC, 32 + TT], F32, 'gbuf')
    acc, b_acc = a([128, KC, TT], F32, 'acc')
    sg, b_sg = [], []
    for i in range(2):
        x, b = a([128, TT], F32, 'sg'); sg.append(x); b_sg.append(b)
    ysT, b_ysT = a([128, KC, TT], BF16, 'ysT')
    mean, b_mean = a([128, TT], F32, 'mean')
    msq, b_msq = a([128, TT], F32, 'msq')
    lrs, b_lrs = a([128, TT], F32, 'lrs')
    t1, b_t1 = a([128, TT], F32, 't1')

    def vec(r):
        return vecT[:, :, r]

    if halo_src is None:
        S.op('pool', lambda e: e.memset(gbuf[:, :, 0:32], 0.0), writes=[b_gbuf])
    else:
        hs, b_hs = halo_src
        S.op('dve', lambda e: e.tensor_scalar(out=gbuf[:, :, 0:32], in0=hs, scalar1=flag_ap, scalar2=None, op0=ALU.mult),
             reads=[b_hs, b_tab], writes=[b_gbuf])

    def glu_tile(c0, W):
        kb.rmsnorm_to(ar, hT, b_hT, c0, W, vec(V_MIX + 0), b_vec, xnT, b_xnT, tmp=(sq, b_sq, rstd, b_rstd))
        for n in range(KC):
            ba, bb = kb.bank(), kb.bank()
            pa = kb.ps[ba][:, 0:W]
            pb = kb.ps[bb][:, 0:W]
            kb.lin(pa, kb.b_ps[ba], w_in, b_R, n * 128, 128, xnT, b_xnT, W)
            kb.lin(pb, kb.b_ps[bb], w_in, b_R, 1024 + n * 128, 128, xnT, b_xnT, W)
            k = n % 2
            S.op('act', lambda e: e.activation(out=sg[k][:, 0:W], in_=pb, func=AF.Sigmoid, bias=vecT[:, n, V_BIN + 1:V_BIN + 2]),
                 reads=[kb.b_ps[bb], b_vec], writes=[b_sg[k]])
            S.op('dve', lambda e: e.scalar_tensor_tensor(out=gbuf[:, n, 32:32 + W], in0=pa, scalar=vecT[:, n, V_BIN:V_BIN + 1], in1=sg[k][:, 0:W],
                                                         op0=ALU.add, op1=ALU.mult),
                 reads=[kb.b_ps[ba], b_sg[k], b_vec], writes=[b_gbuf])

    def ln_out_tile(c0, W):
        S.op('act', lambda e: e.activation(out=sq[:, :, 0:W], in_=acc[:, :, 0:W], func=AF.Square), reads=[b_acc], writes=[b_sq])
        p1 = kb.ps[7][:, 0:W]
        p2 = kb.ps[7][:, 256:256 + W]
        for kc in range(KC):
            S.op('pe', lambda e: e.matmul(p1, lhsT=kb.ones_f[:], rhs=acc[:, kc, 0:W], start=(kc == 0), stop=(kc == KC - 1)),
                 reads=[b_acc, kb.b_const], writes=[kb.b_ps[7]])
        for kc in range(KC):
            S.op('pe', lambda e: e.matmul(p2, lhsT=kb.ones_f[:], rhs=sq[:, kc, 0:W], start=(kc == 0), stop=(kc == KC - 1)),
                 reads=[b_sq, kb.b_const], writes=[kb.b_ps[7]])
        S.op('act', lambda e: e.activation(out=mean[:, 0:W], in_=p1, func=AF.Copy, scale=1.0 / D), reads=[kb.b_ps[7]], writes=[b_mean])
        S.op('dve', lambda e: e.tensor_tensor(out=msq[:, 0:W], in0=mean[:, 0:W], in1=mean[:, 0:W], op=ALU.mult), reads=[b_mean], writes=[b_msq])
        S.op('dve', lambda e: e.scalar_tensor_tensor(out=lrs[:, 0:W], in0=p2, scalar=1.0 / D, in1=msq[:, 0:W], op0=ALU.mult, op1=ALU.subtract),
             reads=[kb.b_ps[7], b_msq], writes=[b_lrs])
        S.op('act', lambda e: e.activation(out=lrs[:, 0:W], in_=lrs[:, 0:W], func=AF.Sqrt, bias=1e-6), reads=[b_lrs], writes=[b_lrs])
        S.op('dve', lambda e: e.reciprocal(out=lrs[:, 0:W], in_=lrs[:, 0:W]), reads=[b_lrs], writes=[b_lrs])
        for n in range(KC):
            S.op('dve', lambda e: e.tensor_tensor(out=t1[:, 0:W], in0=acc[:, n, 0:W], in1=mean[:, 0:W], op=ALU.subtract),
                 reads=[b_acc, b_mean], writes=[b_t1])
            S.op('dve', lambda e: e.tensor_tensor(out=t1[:, 0:W], in0=t1[:, 0:W], in1=lrs[:, 0:W], op=ALU.mult), reads=[b_t1, b_lrs], writes=[b_t1])
            S.op('act', lambda e: e.activation(out=ysT[:, n, 0:W], in_=t1[:, 0:W], func=AF.Silu, scale=vecT[:, n, V_LNG:V_LNG + 1],
                                               bias=vecT[:, n, V_LNB:V_LNB + 1]),
                 reads=[b_t1, b_vec], writes=[b_ysT])
        for n in range(KC):
            bi = kb.bank()
            po = kb.ps[bi][:, 0:W]
            kb.lin(po, kb.b_ps[bi], w_out, b_R, n * 128, 128, ysT, b_ysT, W)
            S.op('dve', lambda e: e.scalar_tensor_tensor(out=hT[:, n, c0:c0 + W], in0=po, scalar=vecT[:, n, V_BOUT:V_BOUT + 1], in1=hT[:, n, c0:c0 + W],
                                                         op0=ALU.add, op1=ALU.add),
                 reads=[kb.b_ps[bi], b_vec, b_hT], writes=[b_hT])

    for (c0, W) in tiles:
        glu_tile(c0, W)
        for n in range(KC):
            S.op('dve', lambda e: e.tensor_scalar(out=acc[:, n, 0:W], in0=gbuf[:, n, 2:2 + W], scalar1=vecT[:, n, V_DWW:V_DWW + 1],
                                                  scalar2=vecT[:, n, V_DWB:V_DWB + 1], op0=ALU.mult, op1=ALU.add),
                 reads=[b_gbuf, b_vec], writes=[b_acc])
            for w in range(1, 31):
                S.op('dve', lambda e: e.scalar_tensor_tensor(out=acc[:, n, 0:W], in0=gbuf[:, n, 2 + w:2 + w + W], scalar=vecT[:, n, V_DWW + w:V_DWW + w + 1],
                                                             in1=acc[:, n, 0:W], op0=ALU.mult, op1=ALU.add),
                     reads=[b_gbuf, b_vec, b_acc], writes=[b_acc])
        ln_out_tile(c0, W)
        S.op('pool', lambda e: e.tensor_copy(out=gbuf[:, :, 0:32], in_=gbuf[:, :, W:W + 32]), reads=[b_gbuf], writes=[b_gbuf])
    if tail_save is not None:
        S.op('pool', lambda e: e.tensor_copy(out=tail_save[0], in_=gbuf[:, :, 0:32]), reads=[b_gbuf], writes=[tail_save[1]])
    if tail_out is not None:
        kb.store_T(lambda kc: gbuf[:, kc, 2:32], b_gbuf, 30, tail_out[0], tail_out[1])

    if smp is not None:
        c0 = smp['c0']
        W = NS
        histT, b_histT = a([128, KC, NS * 30], F32, 'histT')
        sc = smp['state_conv']
        sc_rows = sc.rearrange('s w d -> (s w) d')
        for r0 in range(0, NS * 30, 120):
            kb.load_T(sc_rows[r0:r0 + 120, :], 120, lambda g0, nk: histT[:, g0:g0 + nk, r0:r0 + 120], b_histT)
        glu_tile(c0, W)
        gs = gbuf[:, :, 32:32 + W]
        tmp, b_tmp = histT, b_histT
        dwbase = vecT[:, 0, V_DWW:V_DWW + 1]
        S.op('dve', lambda e: e.tensor_tensor(out=tmp.rearrange('p k (s w) -> p k s w', s=NS), in0=histT.rearrange('p k (s w) -> p k s w', s=NS),
                                              in1=mkap(dwbase, 0, [[NVEC, KC], [0, NS], [1, 30]]), op=ALU.mult),
             reads=[b_histT, b_vec], writes=[b_tmp])
        S.op('dve', lambda e: e.tensor_reduce(out=acc[:, :, 0:W], in_=tmp.rearrange('p k (s w) -> p k s w', s=NS), axis=AX.X, op=ALU.add),
             reads=[b_tmp], writes=[b_acc])
        for n in range(KC):
            S.op('dve', lambda e: e.scalar_tensor_tensor(out=acc[:, n, 0:W], in0=gbuf[:, n, 32:32 + W], scalar=vecT[:, n, V_DWW + 30:V_DWW + 31],
                                                         in1=acc[:, n, 0:W], op0=ALU.mult, op1=ALU.add),
                 reads=[b_gbuf, b_vec, b_acc], writes=[b_acc])
            S.op('dve', lambda e: e.tensor_scalar(out=acc[:, n, 0:W], in0=acc[:, n, 0:W], scalar1=vecT[:, n, V_DWB:V_DWB + 1], scalar2=None, op0=ALU.add),
                 reads=[b_acc, b_vec], writes=[b_acc])
        ln_out_tile(c0, W)
        outs = smp['out_state']
        b_o = smp['b_out']
        S.dma('sp', lambda e: e.dma_start(out=outs[:, 0:29, :], in_=sc[:, 1:30, :]), writes=[b_o], final=True)
        kb.store_T(lambda kc: gbuf[:, kc, 32:32 + W], b_gbuf, W, outs[:, 29, :], b_o)
    return


def ple_phase(kb, ar, R, b_R, vecT, b_vec, layer, hT, b_hT, tiles, wg_d, wp_d, p_src):
    S = kb.S
    ar.phase()
    kb.set_xtok(ar)
    a = ar.alloc
    wg = R[:, 0:KC * 1024].rearrange('p (k n) -> p k n', k=KC)
    wp = R[:, KC * 1024:KC * 1024 + 2 * 1024].rearrange('p (k n) -> p k n', k=2)
    kb.load_w(wg, b_R, wg_d)
    kb.load_w(wp, b_R, wp_d, nkc=2)
    xnT, b_xnT = a([128, KC, TT], BF16, 'xnT')
    sq, b_sq = a([128, KC, TT], F32, 'sq')
    rstd, b_rstd = a([128, TT], F32, 'rstd')
    pT, b_pT = a([128, 2, TT], BF16, 'pT')
    sg, b_sg, tm, b_tm = [], [], [], []
    for i in range(2):
        x, b = a([128, TT], F32, 'sg'); sg.append(x); b_sg.append(b)
        x, b = a([128, TT], F32, 'tm'); tm.append(x); b_tm.append(b)
    for (c0, W) in tiles:
        kb.rmsnorm_to(ar, hT, b_hT, c0, W, vecT[:, :, V_PLE + layer], b_vec, xnT, b_xnT, tmp=(sq, b_sq, rstd, b_rstd))
        for (rows, P, off) in p_src(c0, W):
            kb.load_T(rows, P, lambda g0, nk: pT[:, g0:g0 + nk, off:off + P], b_pT, ncols=256)
        for n in range(KC):
            ba, bb = kb.bank(), kb.bank()
            pa = kb.ps[ba][:, 0:W]
            pb = kb.ps[bb][:, 0:W]
            kb.lin(pa, kb.b_ps[ba], wg, b_R, n * 128, 128, xnT, b_xnT, W)
            kb.lin(pb, kb.b_ps[bb], wp, b_R, n * 128, 128, pT, b_pT, W, nkc=2)
            k = n % 2
            S.op('act', lambda e: e.activation(out=sg[k][:, 0:W], in_=pa, func=AF.Sigmoid), reads=[kb.b_ps[ba]], writes=[b_sg[k]])
            S.op('dve', lambda e: e.tensor_tensor(out=tm[k][:, 0:W], in0=sg[k][:, 0:W], in1=pb, op=ALU.mult),
                 reads=[b_sg[k], kb.b_ps[bb]], writes=[b_tm[k]])
            S.op('pool', lambda e: e.tensor_tensor(out=hT[:, n, c0:c0 + W], in0=hT[:, n, c0:c0 + W], in1=tm[k][:, 0:W], op=ALU.add),
                 reads=[b_hT, b_tm[k]], writes=[b_hT])


def qkv_phase(kb, ar, R, b_R, vecT, b_vec, hT, b_hT, tiles, wqkv_d, KT_scr, V_scr, b_kv, QT_scr=None, k_out=None, v_out=None, b_out=None,
              smp=None):
    S = kb.S
    ar.phase()
    kb.set_xtok(ar)
    a = ar.alloc
    wqkv = R[:, 0:KC * 3072].rearrange('p (k n) -> p k n', k=KC)
    kb.load_w(wqkv[:, :, 0:1536], b_R, wqkv_d[:, 0:1536])
    kb.load_w(wqkv[:, :, 1536:3072], b_R, wqkv_d[:, 1536:3072])
    xnT, b_xnT = a([128, KC, TT], BF16, 'xnT')
    sq, b_sq = a([128, KC, TT], F32, 'sq')
    rstd, b_rstd = a([128, TT], F32, 'rstd')
    ft, b_ft = [], []
    for i in range(2):
        x, b = a([128, TT], BF16, 'ft'); ft.append(x); b_ft.append(b)
    tokf, b_tokf = [], []
    for i in range(2):
        x, b = a([128, 1024], F32, 'tokf'); tokf.append(x); b_tokf.append(b)
    tokb, b_tokb = [], []
    for i in range(2):
        x, b = a([128, 1024], BF16, 'tokb'); tokb.append(x); b_tokb.append(b)
    nft = 0
    ntk = 0
    for (c0, W, row0) in tiles:
        kb.rmsnorm_to(ar, hT, b_hT, c0, W, vecT[:, :, V_MIX + 1], b_vec, xnT, b_xnT, tmp=(sq, b_sq, rstd, b_rstd))
        is_smp = row0 is None
        if not is_smp:
            groups = ([(0, QT_scr, 0.125)] if QT_scr is not None else []) + [(1024, KT_scr, 1.0)]
            for (coff, scr, scl) in groups:
                for hp in range(8):
                    bi = kb.bank()
                    po = kb.ps[bi][:, 0:W]
                    kb.lin(po, kb.b_ps[bi], wqkv, b_R, coff + hp * 128, 128, xnT, b_xnT, W)
                    k = nft % 2
                    nft += 1
                    S.op('act', lambda e: e.activation(out=ft[k][:, 0:W], in_=po, func=AF.Copy, scale=scl), reads=[kb.b_ps[bi]], writes=[b_ft[k]])
                    S.dma('sp', lambda e: e.dma_start(out=scr[hp, :, row0:row0 + W], in_=ft[k][:, 0:W]), reads=[b_ft[k]], writes=[b_kv])
        nsub = (W + 127) // 128
        for sub in range(nsub):
            P = min(128, W - sub * 128)
            t0 = sub * 128
            which = [(1024, 'k'), (2048, 'v')] + ([(0, 'q')] if is_smp else [])
            for (coff, nm) in which:
                k = ntk % 2
                ntk += 1
                if is_smp:
                    dstf = smp[nm + 'tok']
                    b_dstf = smp['b_' + nm]
                else:
                    dstf = tokf[k]
                    b_dstf = b_tokf[k]
                for half in range(2):
                    bi = kb.bank()
                    po = kb.ps[bi][0:P, :]
                    for kc in range(KC):
                        S.op('pe', lambda e: e.matmul(po, lhsT=xnT[:, kc, t0:t0 + P], rhs=wqkv[:, kc, coff + half * 512:coff + (half + 1) * 512],
                                                      start=(kc == 0), stop=(kc == KC - 1)),
                             reads=[b_xnT, b_R], writes=[kb.b_ps[bi]])
                    scl = 0.125 if nm == 'q' else 1.0
                    if k == 0:
                        S.op('act', lambda e: e.activation(out=dstf[0:P, half * 512:(half + 1) * 512], in_=po, func=AF.Copy, scale=scl),
                             reads=[kb.b_ps[bi]], writes=[b_dstf])
                    else:
                        S.op('dve', lambda e: e.tensor_scalar(out=dstf[0:P, half * 512:(half + 1) * 512], in0=po, scalar1=scl, scalar2=None, op0=ALU.mult),
                             reads=[kb.b_ps[bi]], writes=[b_dstf])
                if is_smp:
                    if nm in ('k', 'v'):
                        S.dma('sp', lambda e: e.dma_start(out=smp[nm + '_out'], in_=dstf[0:P, :]), reads=[b_dstf], writes=[smp['b_out']], final=True)
                else:
                    r = row0 + t0
                    if k_out is not None:
                        S.dma('sp', lambda e: e.dma_start(out=(k_out if nm == 'k' else v_out)[r:r + P, :], in_=dstf[0:P, :]),
                              reads=[b_dstf], writes=[b_out], final=True)
                    if nm == 'v':
                        S.op('pool', lambda e: e.tensor_copy(out=tokb[k][0:P, :], in_=dstf[0:P, :]), reads=[b_dstf], writes=[b_tokb[k]])
                        S.dma('sp', lambda e: e.dma_start(out=V_scr[r:r + P, :], in_=tokb[k][0:P, :]), reads=[b_tokb[k]], writes=[b_kv])


def attn_prompt(kb, ar, OT, b_OT, KT_ctx, KT_own, V_ctx, V_own, QT_scr, b_kv, band_d, c31, b_tab, vmask, notown):
    S = kb.S
    ar.phase()
    a = ar.alloc
    KTh, b_KTh = a([128, 4096], BF16, 'KTh')
    QTh, b_QTh = a([128, 2048], BF16, 'QTh')
    Vh, b_Vh = a([128, 32, 65], BF16, 'Vh')
    Bh, b_Bh = a([128, 1024], BF16, 'Bh')
    ksum, b_ksum = a([128, 16], F32, 'ksum')
    ksumb, b_ksumb = a([128, 16], BF16, 'ksumb')
    gm, b_gm = a([128, 16], F32, 'gm')
    top8, b_top8 = a([128, 8], F32, 'top8')
    thr, b_thr = a([128, 1], F32, 'thr')
    negm, b_negm = a([128, 16], F32, 'negm')
    negT, b_negT = a([128, 512], BF16, 'negT')
    Ind, b_Ind = a([128, 16, 128], BF16, 'Ind')
    PT, b_PT = [], []
    for i in range(2):
        x, b = a([128, 512], BF16, 'PT'); PT.append(x); b_PT.append(b)
    Otok, b_Otok = a([128, 16, 128], BF16, 'Otok')
    rden, b_rden = a([128, 4], F32, 'rden')
    S.op('dve', lambda e: e.tensor_copy(out=Ind[0:16], in_=mkap(kb.ident_f[0:16, 0:1], 0, [[1, 16], [0, 128]])),
         reads=[kb.b_const], writes=[b_Ind])
    S.op('pool', lambda e: e.memset(Vh[:, :, 64:65], 1.0), writes=[b_Vh])
    nPT = 0
    for h in range(16):
        hp, r0 = h // 2, (h % 2) * 64
        S.dma('sp', lambda e: e.dma_start(out=KTh[0:64, 0:2048], in_=KT_ctx[hp, r0:r0 + 64, :]), reads=[b_kv], writes=[b_KTh])
        S.dma('sp', lambda e: e.dma_start(out=KTh[0:64, 2048:4096], in_=KT_own[hp, r0:r0 + 64, 0:2048]), reads=[b_kv], writes=[b_KTh])
        S.dma('sp', lambda e: e.dma_start(out=QTh[0:64, :], in_=QT_scr[hp, r0:r0 + 64, 0:2048]), reads=[b_kv], writes=[b_QTh])
        S.dma('sp', lambda e: e.dma_start(out=Vh[:, 0:16, 0:64], in_=V_ctx.rearrange('(kt p) c -> p kt c', p=128)[:, :, h * 64:(h + 1) * 64]),
              reads=[b_kv], writes=[b_Vh])
        S.dma('sp', lambda e: e.dma_start(out=Vh[:, 16:32, 0:64], in_=V_own.rearrange('(kt p) c -> p kt c', p=128)[:, :, h * 64:(h + 1) * 64]),
              reads=[b_kv], writes=[b_Vh])
        S.dma('pool', lambda e: e.dma_start(out=Bh, in_=band_d[h]), writes=[b_Bh])
        S.op('dve', lambda e: e.tensor_reduce(out=ksum[0:64, :], in_=KTh[0:64, :].rearrange('p (b k) -> p b k', b=16), axis=AX.X, op=ALU.add),
             reads=[b_KTh], writes=[b_ksum])
        S.op('dve', lambda e: e.tensor_copy(out=ksumb[0:64, :], in_=ksum[0:64, :]), reads=[b_ksum], writes=[b_ksumb])
        for g in range(4):
            q0 = 512 * g
            for j in range(4):
                qt = 4 * g + j
                pg = kb.ps[7]
                S.op('pe', lambda e: e.matmul(pg[:, 0:16], lhsT=QTh[0:64, qt * 128:(qt + 1) * 128], rhs=ksumb[0:64, :], start=True, stop=True),
                     reads=[b_QTh, b_ksumb], writes=[kb.b_ps[7]])
                S.op('dve', lambda e: e.tensor_tensor(out=gm, in0=pg[:, 0:16], in1=vmask[:, qt, :], op=ALU.add), reads=[kb.b_ps[7], b_tab], writes=[b_gm])
                S.op('dve', lambda e: e.max(out=top8, in_=gm), reads=[b_gm], writes=[b_top8])
                S.op('dve', lambda e: e.tensor_single_scalar(out=thr, in_=top8[:, 2:3], scalar=-1e30, op=ALU.max), reads=[b_top8], writes=[b_thr])
                S.op('dve', lambda e: e.tensor_scalar(out=negm, in0=gm, scalar1=thr[:, 0:1], scalar2=-30000.0, op0=ALU.is_lt, op1=ALU.mult),
                     reads=[b_gm, b_thr], writes=[b_negm])
                S.op('dve', lambda e: e.tensor_tensor(out=negm, in0=negm, in1=notown[:, qt, :], op=ALU.mult), reads=[b_negm, b_tab], writes=[b_negm])
                S.op('pe', lambda e: e.transpose(out=pg[0:16, 128:256], in_=negm, identity=kb.ident_f[:]), reads=[b_negm, kb.b_const], writes=[kb.b_ps[7]])
                S.op('act', lambda e: e.activation(out=negT[0:16, j * 128:(j + 1) * 128], in_=pg[0:16, 128:256], func=AF.Copy),
                     reads=[kb.b_ps[7]], writes=[b_negT])
            nkt = 16 + 4 * g + 4
            pob = 5 + (g % 2)
            pO = kb.ps[pob]
            last_kt = [min(nkt - 1, 16 + 4 * g + j) for j in range(4)]
            for kt in range(nkt):
                near = kt >= 15 + 4 * g
                bi = kb.bank()
                pS = kb.ps[bi]
                S.op('pe', lambda e: e.matmul(pS[:, :], lhsT=KTh[0:64, kt * 128:(kt + 1) * 128], rhs=QTh[0:64, q0:q0 + 512], start=True, stop=False),
                     reads=[b_KTh, b_QTh], writes=[kb.b_ps[bi]])
                S.op('pe', lambda e: e.matmul(pS[:, :], lhsT=Ind[0:16, kt // 2, :], rhs=negT[0:16, :], start=False, stop=(not near)),
                     reads=[b_Ind, b_negT], writes=[kb.b_ps[bi]])
                if near:
                    delta = (2048 + q0) - 128 * kt
                    off = delta + 384
                    S.op('pe', lambda e: e.matmul(pS[:, :], lhsT=kb.ident_b[:], rhs=Bh[:, off:off + 512], start=False, stop=True),
                         reads=[b_Bh, kb.b_const], writes=[kb.b_ps[bi]])
                k = nPT % 2
                nPT += 1
                if near:
                    S.op('act', lambda e: e.activation(out=PT[k], in_=pS[:, :], func=AF.Exp), reads=[kb.b_ps[bi]], writes=[b_PT[k]])
                else:
                    S.op('act', lambda e: e.activation(out=PT[k], in_=pS[:, :], func=AF.Exp, bias=c31[:, h:h + 1]),
                         reads=[kb.b_ps[bi], b_tab], writes=[b_PT[k]])
                for j in range(4):
                    if kt > last_kt[j]:
                        continue
                    S.op('pe', lambda e: e.matmul(pO[:, j * 65:(j + 1) * 65], lhsT=PT[k][:, j * 128:(j + 1) * 128], rhs=Vh[:, kt, :],
                                                  start=(kt == 0), stop=(kt == last_kt[j])),
                         reads=[b_PT[k], b_Vh], writes=[kb.b_ps[pob]])
            pO3 = pO[:, 0:260].rearrange('p (j c) -> p j c', j=4)
            S.op('dve', lambda e: e.reciprocal(out=rden, in_=pO3[:, :, 64]), reads=[kb.b_ps[pob]], writes=[b_rden])
            S.op('dve', lambda e: e.tensor_tensor(out=Otok[:, 4 * g:4 * g + 4, r0:r0 + 64], in0=pO3[:, :, 0:64], in1=mkap(rden[:, 0:1], 0, [[1, 4], [0, 64]]),
                                                  op=ALU.mult),
                 reads=[kb.b_ps[pob], b_rden], writes=[b_Otok])
        if h % 2 == 1:
            for qt in range(16):
                pt = kb.ps[7]
                ptv = pt[:].bitcast(BF16)
                S.op('pe', lambda e: e.transpose(out=ptv[:, 0:128], in_=Otok[:, qt, :], identity=kb.ident_b[:]),
                     reads=[b_Otok, kb.b_const], writes=[kb.b_ps[7]])
                S.op('act', lambda e: e.activation(out=OT[:, hp, qt * 128:(qt + 1) * 128], in_=ptv[:, 0:128], func=AF.Copy),
                     reads=[kb.b_ps[7]], writes=[b_OT])


def attn_sample(kb, ar, OT, b_OT, smp, ck_rows, cv_rows, pt_d, sbias, b_tab, o_scr, b_oscr):
    S = kb.S
    ar.phase()
    kb.set_xtok(ar)
    a = ar.alloc
    ptb, b_ptb = a([128, 256], I32, 'ptb')
    ptf, b_ptf = a([128, 256], F32, 'ptf')
    idx, b_idx = a([128, 256], I32, 'idx')
    Kpg, b_Kpg, Vpg, b_Vpg, Vb, b_Vb = [], [], [], [], [], []
    for i in range(2):
        x, b = a([128, 1024], F32, 'Kpg'); Kpg.append(x); b_Kpg.append(b)
        x, b = a([128, 1024], F32, 'Vpg'); Vpg.append(x); b_Vpg.append(b)
        x, b = a([128, 1024], BF16, 'Vb'); Vb.append(x); b_Vb.append(b)
    prod, b_prod = a([128, 1024], F32, 'prod')
    qb, b_qb = a([128, 1024], F32, 'qb')
    lg, b_lg = a([128, 17, 16], F32, 'lg')
    Pb, b_Pb = a([128, 17, 16], BF16, 'Pb')
    gsum, b_gsum = a([128, 128], F32, 'gsum')
    cmp, b_cmp = prod, b_prod
    cnt, b_cnt = a([128, 128], F32, 'cnt')
    negrow, b_negrow = a([128, 128], F32, 'negrow')
    SelA, b_SelA = a([128, 16, 128], F32, 'SelA')
    E0, b_E0 = a([128, 16, 128], F32, 'E0')
    otmp, b_otmp = qb, b_qb
    osb, b_osb = a([128, 64], F32, 'osb')
    rd, b_rd = a([128, 1], F32, 'rd')
    S.dma('sp', lambda e: e.dma_start(out=ptb, in_=pt_d.rearrange('s g -> (s g)').partition_broadcast(128)), writes=[b_ptb])
    S.op('dve', lambda e: e.tensor_copy(out=ptf, in_=ptb), reads=[b_ptb], writes=[b_ptf])
    S.op('dve', lambda e: e.tensor_scalar(out=ptf, in0=ptf, scalar1=128.0, scalar2=kb.iota_p[:, 0:1], op0=ALU.mult, op1=ALU.add),
         reads=[b_ptf, kb.b_const], writes=[b_ptf])
    S.op('dve', lambda e: e.tensor_copy(out=idx, in_=ptf), reads=[b_ptf], writes=[b_idx])
    S.op('dve', lambda e: e.tensor_copy(out=SelA[0:16], in_=mkap(kb.ident_f[0:16, 0:1], 0, [[1, 16], [0, 128]])), reads=[kb.b_const], writes=[b_SelA])
    S.op('pool', lambda e: e.memset(E0[0:16], 0.0), writes=[b_E0])
    S.op('dve', lambda e: e.tensor_copy(out=E0[0:16, :, 0], in_=kb.ident_f[0:16, 0:16]), reads=[kb.b_const], writes=[b_E0])
    qtok, ktok, vtok = smp['qtok'], smp['ktok'], smp['vtok']
    b_q, b_k, b_v = smp['b_q'], smp['b_k'], smp['b_v']
    import os
    SS = int(os.environ.get('SSTAGE', '99'))
    npg = 0
    if SS < 1:
        return
    for s in range(int(os.environ.get('NSMP', NS))):
        for half in range(2):
            bi = kb.bank()
            po = kb.ps[bi]
            S.op('pe', lambda e: e.matmul(po[:, :], lhsT=SelA[0:16, s, :], rhs=qtok[0:16, half * 512:(half + 1) * 512], start=True, stop=True),
                 reads=[b_SelA, b_q], writes=[kb.b_ps[bi]])
            S.op('act', lambda e: e.activation(out=qb[:, half * 512:(half + 1) * 512], in_=po[:, :], func=AF.Copy), reads=[kb.b_ps[bi]], writes=[b_qb])
        for pg in range(17):
            k = npg % 2
            npg += 1
            if pg < 16:
                col = s * 16 + pg
                S.dma('pool', lambda e: e.indirect_dma_start(out=Kpg[k], out_offset=None, in_=ck_rows,
                                                             in_offset=bass.IndirectOffsetOnAxis(ap=idx[:, col:col + 1], axis=0)),
                      reads=[b_idx], writes=[b_Kpg[k]])
            else:
                for half in range(2):
                    bi = kb.bank()
                    po = kb.ps[bi]
                    S.op('pe', lambda e: e.matmul(po[:, :], lhsT=E0[0:16, s, :], rhs=ktok[0:16, half * 512:(half + 1) * 512], start=True, stop=True),
                         reads=[b_E0, b_k], writes=[kb.b_ps[bi]])
                    S.op('act', lambda e: e.activation(out=Kpg[k][:, half * 512:(half + 1) * 512], in_=po[:, :], func=AF.Copy),
                         reads=[kb.b_ps[bi]], writes=[b_Kpg[k]])
            S.op('pool', lambda e: e.tensor_tensor(out=prod, in0=Kpg[k], in1=qb, op=ALU.mult), reads=[b_Kpg[k], b_qb], writes=[b_prod])
            S.op('dve', lambda e: e.tensor_reduce(out=lg[:, pg, :], in_=prod.rearrange('p (h d) -> p h d', h=16), axis=AX.X, op=ALU.add),
                 reads=[b_prod], writes=[b_lg])
        if SS < 2:
            continue
        pg_ = kb.ps[7]
        S.op('pe', lambda e: e.matmul(pg_[0:1, 0:256], lhsT=kb.ones_f[:, 0:1], rhs=lg[:, 0:16, :].rearrange('p g h -> p (g h)'), start=True, stop=True),
             reads=[b_lg, kb.b_const], writes=[kb.b_ps[7]])
        ps4 = pg_[0:1, 0:256].rearrange('p (b t h) -> p b t h', b=8, t=2)
        S.op('act', lambda e: e.activation(out=gsum[0:1, :].rearrange('p (b h) -> p b h', b=8), in_=ps4[:, :, 0, :], func=AF.Copy),
             reads=[kb.b_ps[7]], writes=[b_gsum])
        S.op('dve', lambda e: e.tensor_tensor(out=gsum[0:1, :].rearrange('p (b h) -> p b h', b=8), in0=gsum[0:1, :].rearrange('p (b h) -> p b h', b=8),
                                              in1=ps4[:, :, 1, :], op=ALU.add),
             reads=[kb.b_ps[7], b_gsum], writes=[b_gsum])
        g0 = gsum[0:1, 0:1]
        S.op('dve', lambda e: e.tensor_tensor(out=cmp[0:1, :].rearrange('p (h b c) -> p h b c', h=16, b=8),
                                              in0=mkap(g0, 0, [[1, 16], [0, 8], [16, 8]]), in1=mkap(g0, 0, [[1, 16], [16, 8], [0, 8]]), op=ALU.is_gt),
             reads=[b_gsum], writes=[b_cmp])
        S.op('dve', lambda e: e.tensor_reduce(out=cnt[0:1, :], in_=cmp[0:1, :].rearrange('p (m c) -> p m c', c=8), axis=AX.X, op=ALU.add),
             reads=[b_cmp], writes=[b_cnt])
        S.op('dve', lambda e: e.tensor_scalar(out=negrow[0:1, :], in0=cnt[0:1, :], scalar1=3.0, scalar2=-30000.0, op0=ALU.is_ge, op1=ALU.mult),
             reads=[b_cnt], writes=[b_negrow])
        S.op('pe', lambda e: e.matmul(pg_[:, 256:384], lhsT=kb.ones_f[0:1, :], rhs=negrow[0:1, :], start=True, stop=True),
             reads=[b_negrow, kb.b_const], writes=[kb.b_ps[7]])
        lg4 = lg[:, 0:16, :].rearrange('p (b t) h -> p b t h', t=2)
        S.op('dve', lambda e: e.tensor_tensor(out=lg4, in0=lg4, in1=mkap(pg_[:, 256:257], 0, [[1, 8], [0, 2], [8, 16]]), op=ALU.add),
             reads=[b_lg, kb.b_ps[7]], writes=[b_lg])
        S.op('dve', lambda e: e.tensor_tensor(out=lg, in0=lg, in1=sbias, op=ALU.add), reads=[b_lg, b_tab], writes=[b_lg])
        S.op('act', lambda e: e.activation(out=Pb, in_=lg, func=AF.Exp), reads=[b_lg], writes=[b_Pb])
        if SS < 3:
            continue
        pO = [kb.ps[5], kb.ps[6]]
        pD = kb.ps[7]
        for pg in range(17):
            k = npg % 2
            npg += 1
            if pg < 16:
                col = s * 16 + pg
                S.dma('pool', lambda e: e.indirect_dma_start(out=Vpg[k], out_offset=None, in_=cv_rows,
                                                             in_offset=bass.IndirectOffsetOnAxis(ap=idx[:, col:col + 1], axis=0)),
                      reads=[b_idx], writes=[b_Vpg[k]])
                S.op('pool', lambda e: e.tensor_copy(out=Vb[k], in_=Vpg[k]), reads=[b_Vpg[k]], writes=[b_Vb[k]])
            else:
                for half in range(2):
                    bi = kb.bank()
                    po = kb.ps[bi]
                    S.op('pe', lambda e: e.matmul(po[:, :], lhsT=E0[0:16, s, :], rhs=vtok[0:16, half * 512:(half + 1) * 512], start=True, stop=True),
                         reads=[b_E0, b_v], writes=[kb.b_ps[bi]])
                    S.op('act', lambda e: e.activation(out=Vb[k][:, half * 512:(half + 1) * 512], in_=po[:, :], func=AF.Copy),
                         reads=[kb.b_ps[bi]], writes=[b_Vb[k]])
            for half in range(2):
                S.op('pe', lambda e: e.matmul(pO[half][0:16, :], lhsT=Pb[:, pg, :], rhs=Vb[k][:, half * 512:(half + 1) * 512], start=(pg == 0), stop=(pg == 16)),
                     reads=[b_Pb, b_Vb[k]], writes=[kb.b_ps[5 + half]])
            S.op('pe', lambda e: e.matmul(pD[0:16, 400:401], lhsT=Pb[:, pg, :], rhs=kb.ones_b[:, 0:1], start=(pg == 0), stop=(pg == 16)),
                 reads=[b_Pb, kb.b_const], writes=[kb.b_ps[7]])
        if SS < 4:
            continue
        for half in range(2):
            S.op('dve', lambda e: e.tensor_tensor(out=otmp[0:16, half * 512:(half + 1) * 512].rearrange('p (h d) -> p h d', h=8),
                                                  in0=pO[half][0:16, :].rearrange('p (h d) -> p h d', h=8),
                                                  in1=mkap(kb.ident_f[0:16, half * 8:half * 8 + 1], 0, [[1, 8], [0, 64]]), op=ALU.mult),
                 reads=[kb.b_ps[5 + half], kb.b_const], writes=[b_otmp])
        S.op('dve', lambda e: e.tensor_reduce(out=osb[0:16, :], in_=mkap(otmp[0:16, 0:1], 0, [[1, 64], [64, 16]]), axis=AX.X, op=ALU.add),
             reads=[b_otmp], writes=[b_osb])
        S.op('dve', lambda e: e.reciprocal(out=rd[0:16, :], in_=pD[0:16, 400:401]), reads=[kb.b_ps[7]], writes=[b_rd])
        S.op('dve', lambda e: e.tensor_scalar(out=osb[0:16, :], in0=osb[0:16, :], scalar1=rd[0:16, 0:1], scalar2=None, op0=ALU.mult),
             reads=[b_osb, b_rd], writes=[b_osb])
        S.dma('sp', lambda e: e.dma_start(out=o_scr[s].rearrange('(h d) -> h d', h=16), in_=osb[0:16, :]), reads=[b_osb], writes=[b_oscr])
    tmpT, b_tmpT = a([128, KC, NS], F32, 'tmpT')
    kb.load_T(o_scr, NS, lambda g0, nk: tmpT[:, g0:g0 + nk, :], b_tmpT, b_src=b_oscr)
    kb.S.op('dve', lambda e: e.tensor_copy(out=OT[:, :, NP_OWN:NP_OWN + NS], in_=tmpT), reads=[b_tmpT], writes=[b_OT])
    return


def wo_phase(kb, ar, R, b_R, wo_view, OT, b_OT, hT, b_hT, tiles, wo_d):
    S = kb.S
    kb.load_w(wo_view, b_R, wo_d)
    for (c0, W) in tiles:
        for n in range(KC):
            bi = kb.bank()
            po = kb.ps[bi][:, 0:W]
            for kc in range(KC):
                S.op('pe', lambda e: e.matmul(po, lhsT=wo_view[:, kc, n * 128:(n + 1) * 128], rhs=OT[:, kc, c0:c0 + W], start=(kc == 0), stop=(kc == KC - 1)),
                     reads=[b_R, b_OT], writes=[kb.b_ps[bi]])
            S.op('dve', lambda e: e.tensor_tensor(out=hT[:, n, c0:c0 + W], in0=hT[:, n, c0:c0 + W], in1=po, op=ALU.add),
                 reads=[b_hT, kb.b_ps[bi]], writes=[b_hT])


def final_phase(kb, ar, vecT, b_vec, hT, b_hT, tiles_rows, b_out):
    S = kb.S
    ar.phase()
    kb.set_xtok(ar)
    a = ar.alloc
    sq, b_sq = a([128, KC, TT], F32, 'sq')
    rstd, b_rstd = a([128, TT], F32, 'rstd')
    yT, b_yT = a([128, KC, TT], F32, 'yT')
    for (c0, W, rows) in tiles_rows:
        kb.rmsnorm_to(ar, hT, b_hT, c0, W, vecT[:, :, V_FIN], b_vec, yT, b_yT, tmp=(sq, b_sq, rstd, b_rstd))
        for s0 in range(0, W, 128):
            P = min(128, W - s0)
            kb.store_T(lambda kc: yT[:, kc, s0:s0 + P], b_yT, P, rows[s0:s0 + P, :], b_out)


def build_program(stop_after=99, dbg=False, cache_rows=2560 * 128):
    nc = bass.Bass("TRN2", target_bir_lowering=False)

    def din(name, shape, dt=F32):
        return nc.dram_tensor(name, list(shape), dt, kind="ExternalInput").ap()

    def dout(name, shape, dt=F32):
        return nc.dram_tensor(name, list(shape), dt, kind="ExternalOutput").ap()

    def dint(name, shape, dt):
        return nc.dram_tensor(name, list(shape), dt, kind="Internal").ap()

    x_ctx = din('x_ctx', [2048, 1024]); x_own = din('x_own', [2048, 1024]); x_smp = din('x_smp', [NS, 1024])
    p_ctx0 = din('p_ctx0', [2048, 256]); p_own = din('p_own', [2, 2048, 256]); p_smp = din('p_smp', [2, NS, 256])
    state_conv = din('state_conv', [NS, 30, 1024])
    cache_k = din('cache_k', [cache_rows, 1024]); cache_v = din('cache_v', [cache_rows, 1024])
    page_table = din('page_table', [NS, 16], I32)
    vecs = din('vecs', [NVEC, 1024])
    w_in = din('w_in', [1024, 2048]); w_out = din('w_out', [1024, 1024])
    wqkv = din('wqkv', [1024, 3072]); wo = din('wo', [1024, 1024])
    peer_wq = din('peer_wq', [2, 1024, 1024]); sub_keys = din('sub_keys', [2, 8, 2, 128, 64])
    peer_u = din('peer_u', [2, 16384, 1024]); peer_v = din('peer_v', [2, 16384, 1024])
    ple_wp = din('ple_wp', [2, 256, 1024]); ple_wg = din('ple_wg', [2, 1024, 1024])
    flags_d = din('flags', [128, 1]); vmask_d = din('vmask', [128, 16, 16]); notown_d = din('notown', [128, 16, 16])
    c31_d = din('c31', [128, 16]); sbias_d = din('sbias', [128, 17, 16]); band_d = din('band', [16, 128, 1024])

    y_own = dout('y_own', [2048, 1024]); y_smp = dout('y_smp', [NS, 1024])
    conv_tail = dout('conv_tail', [30, 1024]); conv_smp = dout('conv_smp', [NS, 30, 1024])
    k_own = dout('k_own', [2048, 1024]); v_own = dout('v_own', [2048, 1024])
    k_smp = dout('k_smp', [NS, 1024]); v_smp = dout('v_smp', [NS, 1024])
    if dbg:
        dbg_h = dout('dbg_h', [NCOL, 1024])

    uT_scr = [dint('uT_scr%d' % l, [128, 128, 1024], BF16) for l in range(2)]
    v_scr = [dint('v_scr%d' % l, [16384, 1024], BF16) for l in range(2)]
    KT_ctx = dint('KT_ctx', [8, 128, 2048], BF16); KT_own = dint('KT_own', [8, 128, 2048], BF16)
    V_ctx = dint('V_ctx', [2048, 1024], BF16); V_own = dint('V_own', [2048, 1024], BF16)
    QT_scr = dint('QT_scr', [8, 128, 2048], BF16)
    o_scr = dint('o_scr', [NS, 1024], F32)

    with contextlib.ExitStack() as st:
        kb = KB(nc, st)
        S = kb.S
        hT = kb.sb('hT', [128, KC, NCOL], F32); b_hT = Buf('hT')
        R = kb.sb('R', [128, 128 * TT], BF16); b_R = Buf('R')
        vecT = kb.sb('vecT', [128, KC, NVEC], F32); b_vec = Buf('vecT')
        gtail = kb.sb('gtail', [128, KC, 32], F32); b_gtail = Buf('gtail')
        flags = kb.sb('flags_sb', [128, 1], F32)
        vmask = kb.sb('vmask_sb', [128, 16, 16], F32); notown = kb.sb('notown_sb', [128, 16, 16], F32)
        c31 = kb.sb('c31_sb', [128, 16], F32); sbias = kb.sb('sbias_sb', [128, 17, 16], F32)
        b_tab = Buf('tab')
        for (dst, src) in ((flags, flags_d), (vmask, vmask_d), (notown, notown_d), (c31, c31_d), (sbias, sbias_d)):
            S.dma('sp', lambda e: e.dma_start(out=dst[:], in_=src), writes=[b_tab])
        ar = Arena(kb, ARENA_F32)
        ar.phase()
        kb.set_xtok(ar)
        kb.load_T(vecs, NVEC, lambda g0, nk: vecT[:, g0:g0 + nk, :], b_vec)
        pe = Peer(kb, R)
        pe.b_Gt = b_R
        b_scr = [Buf('scr0'), Buf('scr1')]
        b_kv = Buf('kv'); b_out = Buf('out'); b_oscr = Buf('oscr')
        ptiles = [(c0, TT) for c0 in range(0, NP_OWN, TT)]
        stiles = [(NP_OWN, NS)]

        def dump_and_finish():
            if dbg:
                ar.phase(); kb.set_xtok(ar)
                for s0 in range(0, NCOL, 128):
                    P = min(128, NCOL - s0)
                    kb.store_T(lambda kc: hT[:, kc, s0:s0 + P], b_hT, P, dbg_h[s0:s0 + P, :], b_out)
            print("op counts", S.cnt, S.dcnt)
            S.emit()
            return nc

        def load_x(src, c0, n):
            for s0 in range(0, n, 128):
                P = min(128, n - s0)
                kb.load_T(src[s0:s0 + P, :], P, lambda g0, nk: hT[:, g0:g0 + nk, c0 + s0:c0 + s0 + P], b_hT)

        def p_src_fn(p_rows, p_smp_rows):
            def f(c0, W):
                if c0 >= NP_OWN:
                    return [(p_smp_rows, NS, 0)]
                return [(p_rows[c0 + s0:c0 + s0 + 128, :], 128, s0) for s0 in range(0, W, 128)]
            return f

        ar.phase()
        pe.prepass(ar, peer_u[0], peer_v[0], uT_scr[0], v_scr[0], b_scr[0])
        if stop_after <= 0:
            return dump_and_finish()
        ar.phase(); kb.set_xtok(ar)
        load_x(x_ctx, 0, 2048)
        conv_phase(kb, ar, R, b_R, vecT, b_vec, hT, b_hT, ptiles, w_in, w_out, None, None, b_tab, tail_save=(gtail[:], b_gtail))
        if stop_after <= 1:
            return dump_and_finish()
        pe.run_phase(ar, 0, peer_wq[0], sub_keys[0], vecT[:, :, V_FFN + 0], b_vec, ptiles, hT, b_hT, uT_scr[0], v_scr[0], b_scr[0])
        if stop_after <= 2:
            return dump_and_finish()
        ple_phase(kb, ar, R, b_R, vecT, b_vec, 0, hT, b_hT, ptiles, ple_wg[0], ple_wp[0], p_src_fn(p_ctx0, None))
        if stop_after <= 3:
            return dump_and_finish()
        qkv_phase(kb, ar, R, b_R, vecT, b_vec, hT, b_hT, [(c0, W, c0) for (c0, W) in ptiles], wqkv, KT_ctx, V_ctx, b_kv)
        if stop_after <= 4:
            return dump_and_finish()
        ar.phase(); kb.set_xtok(ar)
        load_x(x_own, 0, 2048)
        load_x(x_smp, NP_OWN, NS)
        conv_phase(kb, ar, R, b_R, vecT, b_vec, hT, b_hT, ptiles, w_in, w_out, (gtail[:], b_gtail), flags[:, 0:1], b_tab,
                   smp=dict(c0=NP_OWN, state_conv=state_conv, out_state=conv_smp, b_out=b_out), tail_out=(conv_tail, b_out))
        if stop_after <= 5:
            return dump_and_finish()
        pe.run_phase(ar, 0, peer_wq[0], sub_keys[0], vecT[:, :, V_FFN + 0], b_vec, ptiles + stiles, hT, b_hT, uT_scr[0], v_scr[0], b_scr[0])
        ple_phase(kb, ar, R, b_R, vecT, b_vec, 0, hT, b_hT, ptiles + stiles, ple_wg[0], ple_wp[0], p_src_fn(p_own[0], p_smp[0]))
        if stop_after <= 6:
            return dump_and_finish()
        ar.phase()
        pe.prepass(ar, peer_u[1], peer_v[1], uT_scr[1], v_scr[1], b_scr[1])
        Rf = R[:].bitcast(F32)
        smp = dict(c0=NP_OWN, qtok=Rf[:, 12352:13376], ktok=Rf[:, 13376:14400], vtok=Rf[:, 14400:15424],
                   b_q=b_R, b_k=b_R, b_v=b_R, k_out=k_smp, v_out=v_smp, b_out=b_out)
        qkv_phase(kb, ar, R, b_R, vecT, b_vec, hT, b_hT, [(c0, W, c0) for (c0, W) in ptiles] + [(NP_OWN, NS, None)], wqkv, KT_own, V_own, b_kv,
                  QT_scr=QT_scr, k_out=k_own, v_out=v_own, b_out=b_out, smp=smp)
        if stop_after <= 7:
            return dump_and_finish()
        OT = R[:, 0:KC * NCOL].rearrange('p (k t) -> p k t', k=KC)
        b_OT = b_R
        attn_prompt(kb, ar, OT, b_OT, KT_ctx, KT_own, V_ctx, V_own, QT_scr, b_kv, band_d, c31, b_tab, vmask, notown)
        if stop_after <= 8:
            return dump_and_finish()
        attn_sample(kb, ar, OT, b_OT, smp, cache_k, cache_v, page_table, sbias[:], b_tab, o_scr, b_oscr)
        wo_view = R[:, 16512:16512 + KC * 1024].rearrange('p (k n) -> p k n', k=KC)
        wo_phase(kb, ar, R, b_R, wo_view, OT, b_OT, hT, b_hT, ptiles + stiles, wo)
        if stop_after <= 9:
            return dump_and_finish()
        pe.run_phase(ar, 1, peer_wq[1], sub_keys[1], vecT[:, :, V_FFN + 1], b_vec, ptiles + stiles, hT, b_hT, uT_scr[1], v_scr[1], b_scr[1])
        ple_phase(kb, ar, R, b_R, vecT, b_vec, 1, hT, b_hT, ptiles + stiles, ple_wg[1], ple_wp[1], p_src_fn(p_own[1], p_smp[1]))
        if stop_after <= 10:
            return dump_and_finish()
        final_phase(kb, ar, vecT, b_vec, hT, b_hT, [(c0, W, y_own[c0:c0 + W, :]) for (c0, W) in ptiles] + [(NP_OWN, NS, y_smp)], b_out)
        return dump_and_finish()


def _t5_bucket(d):
    d = np.asarray(d, np.int64)
    df = np.maximum(d, 1).astype(np.float32)
    large = 16 + (np.log(df / np.float32(16.0)) / np.float32(np.log(8.0)) * np.float32(16.0)).astype(np.int32)
    large = np.minimum(large, 31)
    return np.where(d < 16, d, large).astype(np.int64)


def make_core_inputs(c, inp):
    seq, half = c // 2, c % 2
    f32 = np.float32
    s0 = NS * c
    xp = inp['x_prompt']
    m = {}
    m['x_ctx'] = np.ascontiguousarray(xp[seq, 0:2048])
    m['x_own'] = np.ascontiguousarray(xp[seq, half * 2048:(half + 1) * 2048])
    m['x_smp'] = np.ascontiguousarray(inp['x_sample'][s0:s0 + NS, 0])
    pp = inp['p_prompt']
    m['p_ctx0'] = np.ascontiguousarray(pp[0, seq, 0:2048])
    m['p_own'] = np.ascontiguousarray(pp[:, seq, half * 2048:(half + 1) * 2048])
    m['p_smp'] = np.ascontiguousarray(inp['p_sample'][:, s0:s0 + NS, 0])
    m['state_conv'] = np.ascontiguousarray(inp['state_conv'][0, s0:s0 + NS])
    m['cache_k'] = inp['cache_k'].reshape(-1, 1024)
    m['cache_v'] = inp['cache_v'].reshape(-1, 1024)
    m['page_table'] = np.ascontiguousarray(inp['page_table'][s0:s0 + NS]).astype(np.int32)
    vecs = np.zeros((NVEC, 1024), f32)
    vecs[V_MIX:V_MIX + 2] = inp['norm_mix_g']; vecs[V_FFN:V_FFN + 2] = inp['norm_ffn_g']; vecs[V_PLE:V_PLE + 2] = inp['norm_ple_g']
    vecs[V_FIN] = inp['norm_final_g']
    vecs[V_BIN:V_BIN + 2] = inp['conv_b_in'][0].reshape(2, 1024)
    vecs[V_DWW:V_DWW + 31] = inp['conv_dw_w'][0]
    vecs[V_DWB] = inp['conv_dw_b'][0]; vecs[V_LNG] = inp['conv_ln_g'][0]; vecs[V_LNB] = inp['conv_ln_b'][0]; vecs[V_BOUT] = inp['conv_b_out'][0]
    m['vecs'] = vecs
    m['w_in'] = inp['conv_w_in'][0]; m['w_out'] = inp['conv_w_out'][0]
    m['wqkv'] = inp['attn_w_qkv'][0]; m['wo'] = inp['attn_w_o'][0]
    m['peer_wq'] = inp['peer_w_q']; m['sub_keys'] = inp['peer_sub_keys']
    m['peer_u'] = inp['peer_u']; m['peer_v'] = inp['peer_v']
    m['ple_wp'] = inp['ple_w_proj']; m['ple_wg'] = inp['ple_w_gate']
    m['flags'] = np.full((128, 1), float(half), f32)
    vm = np.zeros((16, 16), f32); no = np.ones((16, 16), f32)
    for qt in range(16):
        own = 8 + qt // 2
        for b in range(16):
            valid = (b < own) and (b >= 8 or half == 1)
            vm[qt, b] = 0.0 if valid else -2e30
        no[qt, own] = 0.0
    m['vmask'] = np.broadcast_to(vm, (128, 16, 16)).copy()
    m['notown'] = np.broadcast_to(no, (128, 16, 16)).copy()
    rb = inp['rel_bias']
    m['c31'] = np.broadcast_to(rb[31], (128, 16)).copy()
    sb = np.zeros((128, 17, 16), f32)
    p = np.arange(128)
    for pg in range(16):
        dist = 2048 - (pg * 128 + p)
        sb[:, pg, :] = rb[_t5_bucket(dist)]
    sb[:, 16, :] = -30000.0
    sb[0, 16, :] = rb[0]
    m['sbias'] = sb
    k = np.arange(128)[:, None]
    mm = np.arange(1024)[None, :] - 384
    dd = mm - k
    bk = _t5_bucket(np.maximum(dd, 0))
    band = np.empty((16, 128, 1024), f32)
    for h in range(16):
        band[h] = np.where(dd < 0, f32(-30000.0), rb[bk, h])
    m['band'] = band
    return m


_NC_CACHE = {}


def kernel(**inp):
    inp = {k: np.asarray(v) for k, v in inp.items()}
    if 'nc' not in _NC_CACHE:
        _NC_CACHE['nc'] = build_program()
    nc = _NC_CACHE['nc']
    in_maps = [make_core_inputs(c, inp) for c in range(8)]
    res = run_bass_kernel_spmd(nc, in_maps, core_ids=list(range(8)))
    r = res.results
    f32 = np.float32
    y_prompt = np.zeros((4, 4096, 1024), f32); y_sample = np.zeros((128, 1, 1024), f32)
    ncp = np.zeros((1, 4, 30, 1024), f32); ncs = np.zeros((1, 128, 30, 1024), f32)
    kp = np.zeros((1, 4, 4096, 16, 64), f32); vp = np.zeros((1, 4, 4096, 16, 64), f32)
    ks = np.zeros((1, 128, 1, 16, 64), f32); vs = np.zeros((1, 128, 1, 16, 64), f32)
    for c in range(8):
        seq, half = c // 2, c % 2
        sl = slice(half * 2048, (half + 1) * 2048)
        ss = slice(NS * c, NS * (c + 1))
        y_prompt[seq, sl] = r[c]['y_own']
        y_sample[ss, 0] = r[c]['y_smp']
        if half == 1:
            ncp[0, seq] = r[c]['conv_tail']
        ncs[0, ss] = r[c]['conv_smp']
        kp[0, seq, sl] = r[c]['k_own'].reshape(2048, 16, 64)
        vp[0, seq, sl] = r[c]['v_own'].reshape(2048, 16, 64)
        ks[0, ss, 0] = r[c]['k_smp'].reshape(NS, 16, 64)
        vs[0, ss, 0] = r[c]['v_smp'].reshape(NS, 16, 64)
    return (y_prompt, y_sample, ncp, ncs, kp, vp, ks, vs)
```
